# Optimizing a Trainium2 kernel written in Bass

```python
import jax
import jax.numpy as jnp
from jax import lax
import numpy as np

D_MODEL = 1024
BATCH = 4
SEQ = 4096
DEPTH = 4

GRID_W = 64
CTX_LEN = 256
D_FF = 4 * D_MODEL
CHUNK = 64
NORM_EPS = 1e-6
HEAD_NORM_EPS = 1e-5
N_MOD = 6

RWKV_HEAD = 64
RWKV_WIDTH = D_MODEL
RWKV_HEADS = RWKV_WIDTH // RWKV_HEAD
DECAY_LORA = 64
AAA_LORA = 64
MV_LORA = 32
GATE_LORA = 128
RWKV_GN_EPS = 64e-5

GLA_HEADS = 4
GLA_QK = D_MODEL // 2
GLA_V = D_MODEL
GLA_QK_HEAD = GLA_QK // GLA_HEADS
GLA_V_HEAD = GLA_V // GLA_HEADS
GLA_GATE_RANK = 16
GLA_TAU = 16.0

RET_HEADS = 4
RET_QK = D_MODEL
RET_V = D_MODEL
RET_QK_HEAD = RET_QK // RET_HEADS
RET_V_HEAD = RET_V // RET_HEADS
ROPE_BASE = 10000.0

N_BRANCH = 3
RWKV_COLS = (RWKV_WIDTH, RWKV_WIDTH, RWKV_WIDTH, DECAY_LORA, DECAY_LORA, AAA_LORA, GATE_LORA)
GLA_COLS = (GLA_QK, GLA_QK, GLA_V, GLA_V, GLA_GATE_RANK, GLA_GATE_RANK)
RET_COLS = (RET_QK, RET_QK, RET_V, RET_V)
RWKV_IN = 3 * RWKV_WIDTH + 2 * DECAY_LORA + AAA_LORA + GATE_LORA
GLA_IN = 2 * GLA_QK + 2 * GLA_V + 2 * GLA_GATE_RANK
RET_IN = 2 * RET_QK + 2 * RET_V
GATE_IN = N_BRANCH * D_MODEL
D_IN = RWKV_IN + GLA_IN + RET_IN + GATE_IN

kernel_name = 'hybrid_rwkv7_gla_retention_dit'


def split_cols(t, sizes):
    return jnp.split(t, [int(i) for i in np.cumsum(sizes)[:-1]], axis=-1)


def heads(t, h):
    return t.reshape(t.shape[:-1] + (h, t.shape[-1] // h))


def rms_norm(x, g):
    xf = x.astype(jnp.float32)
    y = xf * lax.rsqrt(jnp.mean(jnp.square(xf), -1, keepdims=True) + NORM_EPS)
    return y.astype(x.dtype) * g


def modulate(h, shift, scale):
    return h * (1.0 + scale) + shift


def head_norm(o, gain, bias, eps, center):
    b, t, h, d = o.shape
    of = o.astype(jnp.float32)
    if center:
        of = of - jnp.mean(of, -1, keepdims=True)
    y = of * lax.rsqrt(jnp.mean(jnp.square(of), -1, keepdims=True) + eps)
    y = y.reshape(b, t, h * d).astype(o.dtype) * gain
    return y if bias is None else y + bias


def rotary(t, pos):
    half = t.shape[-1] // 2
    inv_freq = ROPE_BASE ** (-jnp.arange(half, dtype=jnp.float32) / half)
    ang = pos[:, None] * inv_freq[None, :]
    cos = jnp.cos(ang)[None, :, None, :].astype(t.dtype)
    sin = jnp.sin(ang)[None, :, None, :].astype(t.dtype)
    t1, t2 = t[..., :half], t[..., half:]
    return jnp.concatenate([t1 * cos - t2 * sin, t1 * sin + t2 * cos], -1)


def shift_ctx(p):
    b, t, ch = p.shape
    g = p.reshape(b, t, ch // 2, 2)
    prev = jnp.pad(g[:, :-1, :, 0], ((0, 0), (1, 0), (0, 0)))
    nxt = jnp.pad(g[:, 1:, :, 1], ((0, 0), (0, 1), (0, 0)))
    return jnp.stack([prev, nxt], -1).reshape(b, t, ch)


def shift_grid(p, rows):
    b, t, ch = p.shape
    g = p.reshape(b, rows, GRID_W, ch // 4, 4)
    left = jnp.pad(g[:, :, :-1, :, 0], ((0, 0), (0, 0), (1, 0), (0, 0)))
    right = jnp.pad(g[:, :, 1:, :, 1], ((0, 0), (0, 0), (0, 1), (0, 0)))
    up = jnp.pad(g[:, :-1, :, :, 2], ((0, 0), (1, 0), (0, 0), (0, 0)))
    down = jnp.pad(g[:, 1:, :, :, 3], ((0, 0), (0, 1), (0, 0), (0, 0)))
    return jnp.stack([left, right, up, down], -1).reshape(b, t, ch)


def rwkv7_scan(r, decay, k, v, kk, a, s0):
    dt = v.dtype
    xs = tuple(jnp.moveaxis(t.astype(jnp.float32), 1, 0) for t in (r, decay, k, v, kk, a))

    def step(s, inp):
        r_t, w_t, k_t, v_t, kk_t, a_t = inp
        s_kk = jnp.einsum('bhvk,bhk->bhv', s, kk_t)
        s = (s * w_t[:, :, None, :] - s_kk[..., None] * (kk_t * a_t)[:, :, None, :]
             + v_t[..., None] * k_t[:, :, None, :])
        return s, jnp.einsum('bhvk,bhk->bhv', s, r_t)

    s_fin, o = lax.scan(step, s0, xs)
    return jnp.moveaxis(o, 0, 1).astype(dt), s_fin


def chunked_gated_scan(q, k, v, log_g, s0):
    dt = v.dtype
    b, t, h, _ = q.shape
    dv = v.shape[-1]
    n = t // CHUNK
    q, k, v, log_g = (z.astype(jnp.float32).reshape(b, n, CHUNK, h, z.shape[-1]) for z in (q, k, v, log_g))
    cum = jnp.cumsum(log_g, axis=2)
    last = cum[:, :, -1:]
    q_in = q * jnp.exp(cum)
    k_in = k * jnp.exp(-cum)
    k_out = k * jnp.exp(last - cum)
    lower = jnp.tril(jnp.ones((CHUNK, CHUNK), dtype=bool))
    scores = jnp.where(lower, jnp.einsum('bnchd,bnshd->bnhcs', q_in, k_in), 0.0)
    intra = jnp.einsum('bnhcs,bnshv->bnchv', scores, v)
    kv = jnp.einsum('bnshd,bnshv->bnhdv', k_out, v)

    def step(s, inp):
        dec, kv_n = inp
        return dec[..., None] * s + kv_n, s

    s_fin, s_prev = lax.scan(step, s0, (jnp.moveaxis(jnp.exp(last[:, :, 0]), 1, 0), jnp.moveaxis(kv, 1, 0)))
    inter = jnp.einsum('bnchd,nbhdv->bnchv', q_in, s_prev)
    return (intra + inter).reshape(b, t, h, dv).astype(dt), s_fin


def ctx_then_latent(scan_fn, args_c, args_l, s0, reverse):
    if reverse:
        args_c = tuple(jnp.flip(t, 1) for t in args_c)
        args_l = tuple(jnp.flip(t, 1) for t in args_l)
    o_c, s_c = scan_fn(*args_c, s0)
    o_l, _ = scan_fn(*args_l, s_c)
    if reverse:
        o_c, o_l = jnp.flip(o_c, 1), jnp.flip(o_l, 1)
    return o_c, o_l


def bidirectional(scan_fn, fwd_c, fwd_l, bwd_c, bwd_l, s0):
    of_c, of_l = ctx_then_latent(scan_fn, fwd_c, fwd_l, s0, False)
    ob_c, ob_l = ctx_then_latent(scan_fn, bwd_c, bwd_l, s0, True)
    return of_c + ob_c, of_l + ob_l


def rwkv7_stream(p, shifted, v_first, vres, mu, w0, w2, a0, a2, g2, k_k, k_a):
    p = p + (shifted - p) * mu
    r, k, v, wd_f, wd_b, a_d, g_d = split_cols(p, RWKV_COLS)
    if vres is not None:
        v0, v1, v2 = vres
        v = v + (v_first - v) * jax.nn.sigmoid(v0 + (v @ v1) @ v2)
    dec_f, dec_b = (jnp.exp(-jnp.exp(-jax.nn.softplus(-(w0[d] + jnp.tanh(wd) @ w2[d])) - 0.5))
                    for d, wd in enumerate((wd_f, wd_b)))
    a = jax.nn.sigmoid(a0 + a_d @ a2)
    g = jax.nn.sigmoid(g_d) @ g2
    kk = heads(k * k_k, RWKV_HEADS)
    kk = kk * lax.rsqrt(jnp.maximum(jnp.sum(jnp.square(kk.astype(jnp.float32)), -1, keepdims=True), 1e-12)).astype(kk.dtype)
    k = k * (1.0 + (a - 1.0) * k_a)
    hd = lambda t: heads(t, RWKV_HEADS)
    return (hd(r), hd(dec_f), hd(dec_b), hd(k), hd(v), kk, hd(a), g, v)


def rwkv7_mixer(p_c, p_l, rows, v_first, vres, ctx_out, mu, w0, w2, a0, a2, g2, k_k, k_a, r_k, ln_g, ln_b):
    vf_c, vf_l = (None, None) if v_first is None else v_first
    sc = rwkv7_stream(p_c, shift_ctx(p_c), vf_c, vres, mu, w0, w2, a0, a2, g2, k_k, k_a)
    sl = rwkv7_stream(p_l, shift_grid(p_l, rows), vf_l, vres, mu, w0, w2, a0, a2, g2, k_k, k_a)
    s0 = jnp.zeros((p_c.shape[0], RWKV_HEADS, RWKV_HEAD, RWKV_HEAD), jnp.float32)
    fwd = lambda s: (s[0], s[1], s[3], s[4], s[5], s[6])
    bwd = lambda s: (s[0], s[2], s[3], s[4], s[5], s[6])
    o_c, o_l = bidirectional(rwkv7_scan, fwd(sc), fwd(sl), bwd(sc), bwd(sl), s0)

    def finish(o, s):
        r, k, v, g = s[0], s[3], s[4], s[7]
        y = head_norm(o, ln_g, ln_b, RWKV_GN_EPS, True)
        bonus = jnp.sum(r * k * r_k, -1, keepdims=True) * v
        return (y + bonus.reshape(y.shape)) * g

    out_c = finish(o_c, sc) if ctx_out else None
    return out_c, finish(o_l, sl), (sc[8], sl[8])


def gla_stream(p, a2, a_b):
    q, k, v, r, ad_f, ad_b = split_cols(p, GLA_COLS)
    lg_f, lg_b = (jax.nn.log_sigmoid(ad @ a2[d] + a_b[d]) / GLA_TAU for d, ad in enumerate((ad_f, ad_b)))
    hd = lambda t: heads(t, GLA_HEADS)
    return (hd(q * GLA_QK_HEAD ** -0.5), hd(k), hd(v), hd(lg_f), hd(lg_b), r)


def gla_mixer(p_c, p_l, ctx_out, a2, a_b, norm_g):
    sc, sl = gla_stream(p_c, a2, a_b), gla_stream(p_l, a2, a_b)
    s0 = jnp.zeros((p_c.shape[0], GLA_HEADS, GLA_QK_HEAD, GLA_V_HEAD), jnp.float32)
    fwd = lambda s: (s[0], s[1], s[2], s[3])
    bwd = lambda s: (s[0], s[1], s[2], s[4])
    o_c, o_l = bidirectional(chunked_gated_scan, fwd(sc), fwd(sl), bwd(sc), bwd(sl), s0)
    finish = lambda o, s: head_norm(o, norm_g, None, HEAD_NORM_EPS, False) * jax.nn.silu(s[5])
    out_c = finish(o_c, sc) if ctx_out else None
    return out_c, finish(o_l, sl)


def retention_stream(p, pos):
    q, k, v, g = split_cols(p, RET_COLS)
    q = rotary(heads(q, RET_HEADS), pos)
    k = rotary(heads(k, RET_HEADS), pos) * RET_QK_HEAD ** -0.5
    return (q, k, heads(v, RET_HEADS), g)


def retention_mixer(p_c, p_l, ctx_out, decay_param, norm_g):
    n_c, n_l = p_c.shape[1], p_l.shape[1]
    sc = retention_stream(p_c, jnp.arange(n_c, dtype=jnp.float32))
    sl = retention_stream(p_l, n_c + jnp.arange(n_l, dtype=jnp.float32))
    log_gamma = -jnp.exp(decay_param.astype(jnp.float32))
    args = lambda s, d: (s[0], s[1], s[2], jnp.broadcast_to(log_gamma[d][:, None], s[0].shape))
    s0 = jnp.zeros((p_c.shape[0], RET_HEADS, RET_QK_HEAD, RET_V_HEAD), jnp.float32)
    o_c, o_l = bidirectional(chunked_gated_scan, args(sc, 0), args(sl, 0), args(sc, 1), args(sl, 1), s0)
    finish = lambda o, s: head_norm(o, norm_g, None, HEAD_NORM_EPS, True) * jax.nn.silu(s[3])
    out_c = finish(o_c, sc) if ctx_out else None
    return out_c, finish(o_l, sl)


def merge_branches(outs, gate_logits, w_branch, w_out):
    g0, g1, g2 = split_cols(gate_logits, (D_MODEL,) * N_BRANCH)
    o0, o1, o2 = outs
    merged = (jax.nn.sigmoid(g0) * (o0 @ w_branch[0]) + jax.nn.sigmoid(g1) * (o1 @ w_branch[1])
              + jax.nn.sigmoid(g2) * (o2 @ w_branch[2]))
    return merged @ w_out


def squared_relu_mlp(h, w1, w2):
    return jnp.square(jax.nn.relu(h @ w1)) @ w2


def setup_inputs(seed: int = 0) -> dict:
    key = jax.random.key(seed)
    ks = iter(jax.random.split(key, 40))
    nrm = lambda shape, scale: scale * jax.random.normal(next(ks), shape, jnp.float32)
    gain = lambda shape: 1.0 + nrm(shape, 0.01)
    L, D = DEPTH, D_MODEL
    ret_init = np.log(-np.log(1.0 - 2.0 ** (-5.0 - np.arange(RET_HEADS)))).astype(np.float32)
    return {
        'x': nrm((BATCH, SEQ, D), 1.0),
        'c': nrm((BATCH, D), 1.0),
        'ctx': nrm((BATCH, CTX_LEN, D), 1.0),
        'c_ctx': nrm((D,), 1.0),
        'ada_w': nrm((L, D, N_MOD * D), 0.5 * D ** -0.5),
        'ada_b': nrm((L, N_MOD * D), 0.02),
        'norm_pre_mix': gain((L, D)),
        'norm_post_mix': gain((L, D)),
        'norm_pre_mlp': gain((L, D)),
        'norm_post_mlp': gain((L, D)),
        'w_in': nrm((L, D, D_IN), D ** -0.5),
        'rwkv_mu': jax.random.uniform(next(ks), (L, RWKV_IN), jnp.float32, 0.0, 1.0),
        'rwkv_w0': jax.random.uniform(next(ks), (L, 2, RWKV_WIDTH), jnp.float32, -5.0, -1.0),
        'rwkv_w2': nrm((L, 2, DECAY_LORA, RWKV_WIDTH), 0.5 * DECAY_LORA ** -0.5),
        'rwkv_a0': nrm((L, RWKV_WIDTH), 0.5),
        'rwkv_a2': nrm((L, AAA_LORA, RWKV_WIDTH), AAA_LORA ** -0.5),
        'rwkv_g2': nrm((L, GATE_LORA, RWKV_WIDTH), GATE_LORA ** -0.5),
        'rwkv_v0': nrm((L - 1, RWKV_WIDTH), 0.5),
        'rwkv_v1': nrm((L - 1, RWKV_WIDTH, MV_LORA), RWKV_WIDTH ** -0.5),
        'rwkv_v2': nrm((L - 1, MV_LORA, RWKV_WIDTH), MV_LORA ** -0.5),
        'rwkv_k_k': 0.85 + nrm((L, RWKV_WIDTH), 0.05),
        'rwkv_k_a': 1.0 + nrm((L, RWKV_WIDTH), 0.05),
        'rwkv_r_k': nrm((L, RWKV_HEADS, RWKV_HEAD), 0.1),
        'rwkv_ln_g': gain((L, RWKV_WIDTH)),
        'rwkv_ln_b': nrm((L, RWKV_WIDTH), 0.01),
        'gla_a2': nrm((L, 2, GLA_GATE_RANK, GLA_QK), GLA_GATE_RANK ** -0.5),
        'gla_a_b': nrm((L, 2, GLA_QK), 0.1),
        'gla_norm_g': gain((L, GLA_V)),
        'ret_decay': jnp.asarray(ret_init) + nrm((L, 2, RET_HEADS), 0.05),
        'ret_norm_g': gain((L, RET_V)),
        'w_branch': nrm((L, N_BRANCH, D, D), D ** -0.5),
        'w_out': nrm((L, D, D), D ** -0.5),
        'mlp_w1': nrm((L, D, D_FF), D ** -0.5),
        'mlp_w2': nrm((L, D_FF, D), D_FF ** -0.5),
    }


def reference(x, c, ctx, c_ctx, ada_w, ada_b, norm_pre_mix, norm_post_mix, norm_pre_mlp, norm_post_mlp,
              w_in, rwkv_mu, rwkv_w0, rwkv_w2, rwkv_a0, rwkv_a2, rwkv_g2, rwkv_v0, rwkv_v1, rwkv_v2,
              rwkv_k_k, rwkv_k_a, rwkv_r_k, rwkv_ln_g, rwkv_ln_b, gla_a2, gla_a_b, gla_norm_g,
              ret_decay, ret_norm_g, w_branch, w_out, mlp_w1, mlp_w2):
    rows = x.shape[1] // GRID_W
    xl, xc = x, ctx
    v_first = None
    for l in range(DEPTH):
        last = l == DEPTH - 1
        mod_c = split_cols(jax.nn.silu(c_ctx) @ ada_w[l] + ada_b[l], (D_MODEL,) * N_MOD)
        mod_l = split_cols((jax.nn.silu(c) @ ada_w[l] + ada_b[l])[:, None, :], (D_MODEL,) * N_MOD)

        p_c = modulate(rms_norm(xc, norm_pre_mix[l]), mod_c[0], mod_c[1]) @ w_in[l]
        p_l = modulate(rms_norm(xl, norm_pre_mix[l]), mod_l[0], mod_l[1]) @ w_in[l]
        rw_c, gl_c, rt_c, gate_c = split_cols(p_c, (RWKV_IN, GLA_IN, RET_IN, GATE_IN))
        rw_l, gl_l, rt_l, gate_l = split_cols(p_l, (RWKV_IN, GLA_IN, RET_IN, GATE_IN))
        vres = None if l == 0 else (rwkv_v0[l - 1], rwkv_v1[l - 1], rwkv_v2[l - 1])
        a_c, a_l, v_vals = rwkv7_mixer(rw_c, rw_l, rows, v_first, vres, not last, rwkv_mu[l], rwkv_w0[l],
                                       rwkv_w2[l], rwkv_a0[l], rwkv_a2[l], rwkv_g2[l], rwkv_k_k[l],
                                       rwkv_k_a[l], rwkv_r_k[l], rwkv_ln_g[l], rwkv_ln_b[l])
        if l == 0:
            v_first = v_vals
        b_c, b_l = gla_mixer(gl_c, gl_l, not last, gla_a2[l], gla_a_b[l], gla_norm_g[l])
        r_c, r_l = retention_mixer(rt_c, rt_l, not last, ret_decay[l], ret_norm_g[l])
        xl = xl + mod_l[2] * rms_norm(merge_branches((a_l, b_l, r_l), gate_l, w_branch[l], w_out[l]), norm_post_mix[l])

        h_l = modulate(rms_norm(xl, norm_pre_mlp[l]), mod_l[3], mod_l[4])
        xl = xl + mod_l[5] * rms_norm(squared_relu_mlp(h_l, mlp_w1[l], mlp_w2[l]), norm_post_mlp[l])
        if not last:
            xc = xc + mod_c[2] * rms_norm(merge_branches((a_c, b_c, r_c), gate_c, w_branch[l], w_out[l]), norm_post_mix[l])
            h_c = modulate(rms_norm(xc, norm_pre_mlp[l]), mod_c[3], mod_c[4])
            xc = xc + mod_c[5] * rms_norm(squared_relu_mlp(h_c, mlp_w1[l], mlp_w2[l]), norm_post_mlp[l])
    return xl
```

```python
import numpy as np
import concourse.bass as bass
import concourse.mybir as mybir
from concourse.bass_utils import run_bass_kernel_spmd
from contextlib import ExitStack

F32 = mybir.dt.float32
BF16 = mybir.dt.bfloat16
AF = mybir.ActivationFunctionType
ALU = mybir.AluOpType
AX = mybir.AxisListType

D = 1024
KB = 8
GW = 64
ALPHA = float(np.exp(-0.5))


class Res:
    __slots__ = ("name", "w", "r")

    def __init__(self, name):
        self.name = name
        self.w = None
        self.r = {}


class V:
    __slots__ = ("ap", "res")

    def __init__(self, ap, res):
        self.ap = ap
        self.res = res


class T:
    def __init__(self, t, name):
        self.t = t
        self.res = Res(name)

    def __getitem__(self, k):
        return V(self.t[k], self.res)

    def v(self, ap):
        return V(ap, self.res)


class DT:
    def __init__(self, ap, name):
        self.ap = ap
        self.name = name
        self.rs = {}

    def v(self, key, ap):
        r = self.rs.get(key)
        if r is None:
            r = self.rs[key] = Res("%s/%s" % (self.name, key))
        return V(ap, r)


class Prog:
    ENG = ("pe", "dve", "act", "pool", "sp")
    LIMIT = 16000

    def __init__(self, nc, n_dma_slots=6):
        self.nc = nc
        self.eng = {"pe": nc.tensor, "dve": nc.vector, "act": nc.scalar, "pool": nc.gpsimd, "sp": nc.sync}
        self.sem = {}
        self.cnt = {}
        self.keyeng = {}
        self.cur = {}
        self.nkeys = 0
        for e in self.ENG:
            self.cur[e] = self._newkey(e)
        self.dslots = {}
        self.dnext = {}
        for q in ("sp", "pool"):
            self.dslots[q] = [self._newkey("dma") for i in range(n_dma_slots)]
            self.dnext[q] = 0
        self.waited = {e: {} for e in self.ENG}
        self.ninst = 0
        self.stack = None

    def _newkey(self, e):
        self.nkeys += 1
        k = "%s#%d" % (e, self.nkeys)
        self.sem[k] = self.nc.alloc_semaphore(name="s_%s_%d" % (e, self.nkeys))
        self.cnt[k] = 0
        self.keyeng[k] = e
        return k

    def sb(self, name, shape, dt=F32):
        self.uid = getattr(self, "uid", 0) + 1
        name = "sb%d_%s" % (self.uid, name)
        t = self.stack.enter_context(self.nc.sbuf_tensor(name, list(shape), dt))
        return T(t, name)

    def ps(self, name, shape, dt=F32):
        self.uid = getattr(self, "uid", 0) + 1
        name = "ps%d_%s" % (self.uid, name)
        t = self.stack.enter_context(self.nc.psum_tensor(name, list(shape), dt))
        return T(t, name)

    def _wait(self, e, key, val):
        if val <= 0:
            return
        if e == "pe" and self.keyeng[key] == "pe":
            return
        w = self.waited[e]
        if w.get(key, 0) >= val:
            return
        self.eng[e].wait_ge(self.sem[key], val)
        w[key] = val

    def _deps(self, e, reads, writes):
        for r in reads:
            if r.w is not None:
                self._wait(e, r.w[0], r.w[1])
        for r in writes:
            if r.w is not None:
                self._wait(e, r.w[0], r.w[1])
            for k, v in r.r.items():
                self._wait(e, k, v)

    def _mark(self, key, val, reads, writes):
        for r in reads:
            if r.r.get(key, 0) < val:
                r.r[key] = val
        for r in writes:
            r.w = (key, val)
            r.r = {}

    def op(self, e, fn, reads=(), writes=()):
        reads = [x.res for x in reads if x is not None and not isinstance(x, (int, float))]
        writes = [x.res for x in writes]
        self._deps(e, reads, writes)
        ins = fn(self.eng[e])
        k = self.cur[e]
        if self.cnt[k] >= self.LIMIT:
            k = self.cur[e] = self._newkey(e)
        self.cnt[k] += 1
        ins.then_inc(self.sem[k], 1)
        self._mark(k, self.cnt[k], reads, writes)
        self.ninst += 1
        return ins

    def dma(self, out, in_, q="sp", **kw):
        slots = self.dslots[q]
        si = self.dnext[q] % len(slots)
        k = slots[si]
        self.dnext[q] += 1
        self._wait(q, k, self.cnt[k])
        if self.cnt[k] >= self.LIMIT:
            k = slots[si] = self._newkey("dma")
        reads = [in_.res]
        writes = [out.res]
        self._deps(q, reads, writes)
        ins = self.eng[q].dma_start(out=out.ap, in_=in_.ap, **kw)
        self.cnt[k] += 16
        ins.then_inc(self.sem[k], 16)
        self._mark(k, self.cnt[k], reads, writes)
        self.ninst += 1

    def barrier(self):
        for e in self.ENG:
            for k in self.sem:
                self._wait(e, k, self.cnt[k])

    def mm(self, out, lhsT, rhs, start=True, stop=True):
        self.op("pe", lambda e: e.matmul(out.ap, lhsT=lhsT.ap, rhs=rhs.ap, start=start, stop=stop),
                reads=[lhsT, rhs], writes=[out])

    def tr(self, out, in_, ident):
        self.op("pe", lambda e: e.transpose(out=out.ap, in_=in_.ap, identity=ident.ap),
                reads=[in_, ident], writes=[out])

    def act(self, out, in_, func, scale=1.0, bias=None, accum=None, eng="act"):
        kw = {}
        rd = [in_]
        sc = scale
        if isinstance(scale, V):
            sc = scale.ap
            rd.append(scale)
        if bias is not None:
            if isinstance(bias, V):
                kw["bias"] = bias.ap
                rd.append(bias)
            else:
                kw["bias"] = bias
        wr = [out]
        if accum is not None:
            kw["accum_out"] = accum.ap
            wr.append(accum)
        self.op(eng, lambda e: e.activation(out=out.ap, in_=in_.ap, func=func, scale=sc, **kw), reads=rd, writes=wr)

    def tt(self, out, a, b, op, eng="dve"):
        self.op(eng, lambda e: e.tensor_tensor(out=out.ap, in0=a.ap, in1=b.ap, op=op), reads=[a, b], writes=[out])

    def ts(self, out, a, s1, op0, s2=None, op1=None, eng="dve"):
        rd = [a]
        x1 = s1
        if isinstance(s1, V):
            x1 = s1.ap
            rd.append(s1)
        x2 = s2
        if isinstance(s2, V):
            x2 = s2.ap
            rd.append(s2)
        if op1 is None:
            self.op(eng, lambda e: e.tensor_scalar(out=out.ap, in0=a.ap, scalar1=x1, scalar2=None, op0=op0),
                    reads=rd, writes=[out])
        else:
            self.op(eng, lambda e: e.tensor_scalar(out=out.ap, in0=a.ap, scalar1=x1, scalar2=x2, op0=op0, op1=op1),
                    reads=rd, writes=[out])

    def stt(self, out, in0, scalar, in1, op0, op1):
        rd = [in0, in1]
        x = scalar
        if isinstance(scalar, V):
            x = scalar.ap
            rd.append(scalar)
        self.op("dve", lambda e: e.scalar_tensor_tensor(out=out.ap, in0=in0.ap, scalar=x, in1=in1.ap, op0=op0, op1=op1),
                reads=rd, writes=[out])

    def cp(self, out, in_, eng="dve"):
        if eng == "act":
            self.op(eng, lambda e: e.activation(out=out.ap, in_=in_.ap, func=AF.Copy), reads=[in_], writes=[out])
        else:
            self.op(eng, lambda e: e.tensor_copy(out=out.ap, in_=in_.ap), reads=[in_], writes=[out])

    def memset(self, out, val, eng="pool"):
        self.op(eng, lambda e: e.memset(out.ap, val), writes=[out])

    def recip(self, out, in_):
        self.op("dve", lambda e: e.reciprocal(out=out.ap, in_=in_.ap), reads=[in_], writes=[out])

    def scan(self, out, d0, d1, init, op0, op1):
        self.op("dve", lambda e: e.tensor_tensor_scan(out=out.ap, data0=d0.ap, data1=d1.ap, initial=init, op0=op0, op1=op1),
                reads=[d0, d1], writes=[out])

    def reduce(self, out, in_, op=ALU.add, axis=AX.X):
        self.op("dve", lambda e: e.tensor_reduce(out=out.ap, in_=in_.ap, axis=axis, op=op), reads=[in_], writes=[out])

    def finish(self):
        for k in self.sem:
            self._wait("sp", k, self.cnt[k])


PC_MU = 0
PC_W0F = 27
PC_W0B = 35
PC_A0 = 43
PC_V0 = 51
PC_KK = 59
PC_KA = 67
PC_RK = 75
PC_NPM = 83
PC_NPL = 91
PC_SLOT = 99
PC_RETD = 103
PC_FLAG = 111
PC_IDX = 112
PC_RIDX = 113
PC_IDX1 = 114
PC_RIDX1 = 115
NPC = 116

PR_LNG, PR_LNB, PR_GLAG, PR_RETG, PR_NPOM, PR_NPOL = range(6)
NPR = 6

CM_SF, CM_IF, CM_SB, CM_IB, CM_ID, CM_DF, CM_DB, CM_BO = range(8)


class Cfg:
    def __init__(self, S=4096, NC=256):
        self.S = S
        self.NC = NC
        self.NT = S + NC
        self.NTL = self.NT // 128
        self.NCT = NC // 128
        self.ROWS = S // GW
        self.NG = S // 256


def host_consts(cfg):
    idx = np.arange(128)
    s = idx[:, None]
    c = idx[None, :]
    cm = np.zeros((128, 8, 128), np.float32)
    cm[:, CM_SF] = (s < c)
    cm[:, CM_IF] = (s <= c)
    cm[:, CM_SB] = (s > c)
    cm[:, CM_IB] = (s >= c)
    cm[:, CM_ID] = (s == c)
    cm[:, CM_DF] = np.maximum(c - s, 0)
    cm[:, CM_DB] = np.maximum(s - c, 0)
    cm[:, CM_BO] = ((s // 64) == (c // 64))
    half = 128
    inv_freq = (10000.0 ** (-np.arange(half, dtype=np.float32) / half)).astype(np.float32)
    pos = np.arange(cfg.NT, dtype=np.float32)
    ang = (pos[:, None] * inv_freq[None, :]).astype(np.float32)
    cosT = np.ascontiguousarray(np.cos(ang).astype(np.float32).T)
    sinT = np.ascontiguousarray(np.sin(ang).astype(np.float32).T)
    return {"cmat": cm.reshape(128, 8 * 128), "cosT": cosT, "sinT": sinT,
            "cosK": np.ascontiguousarray(cosT.T), "sinK": np.ascontiguousarray(sinT.T)}


def col(vec):
    v = np.asarray(vec, np.float32).reshape(-1, 128)
    return np.ascontiguousarray(v.T)


def host_layer_inputs(inp, l, b, cfg):
    f = lambda a: np.ascontiguousarray(np.asarray(a, np.float32))
    w_in = f(inp["w_in"][l])
    o_rw = 0
    lo = o_rw + 3072
    w_lora = np.zeros((D, 384), np.float32)
    w_lora[:, 0:192] = w_in[:, lo:lo + 192]
    w_lora[:, 256:384] = w_in[:, lo + 192:lo + 320]
    o_gla = 3392
    ga = o_gla + 512 + 512 + 1024 + 1024
    w_ad = np.zeros((D, 64), np.float32)
    w_ad[:, 0:16] = w_in[:, ga:ga + 16]
    w_ad[:, 32:48] = w_in[:, ga + 16:ga + 32]
    mu = f(inp["rwkv_mu"][l])
    mu_l = np.zeros(384, np.float32)
    mu_l[0:192] = mu[3072:3264]
    mu_l[256:384] = mu[3264:3392]
    pc = np.zeros((128, NPC), np.float32)
    pc[:, PC_MU:PC_MU + 24] = col(mu[0:3072])
    pc[:, PC_MU + 24:PC_MU + 27] = col(mu_l)
    pc[:, PC_W0F:PC_W0F + 8] = col(inp["rwkv_w0"][l][0])
    pc[:, PC_W0B:PC_W0B + 8] = col(inp["rwkv_w0"][l][1])
    pc[:, PC_A0:PC_A0 + 8] = col(inp["rwkv_a0"][l])
    if l > 0:
        pc[:, PC_V0:PC_V0 + 8] = col(inp["rwkv_v0"][l - 1])
    pc[:, PC_KK:PC_KK + 8] = col(inp["rwkv_k_k"][l])
    pc[:, PC_KA:PC_KA + 8] = col(inp["rwkv_k_a"][l])
    pc[:, PC_RK:PC_RK + 8] = col(np.asarray(inp["rwkv_r_k"][l]).reshape(-1))
    pc[:, PC_NPM:PC_NPM + 8] = col(inp["norm_pre_mix"][l])
    pc[:, PC_NPL:PC_NPL + 8] = col(inp["norm_pre_mlp"][l])
    for j in range(4):
        pc[:, PC_SLOT + j] = (np.arange(128) % 4 == j)
    rd = np.asarray(inp["ret_decay"][l], np.float32).reshape(-1)
    pc[:, PC_RETD:PC_RETD + 8] = rd[None, :]
    pc[:, PC_FLAG] = 1.0 if l > 0 else 0.0
    pc[:, PC_IDX] = np.arange(128)
    pc[:, PC_RIDX] = 127 - np.arange(128)
    pc[:, PC_IDX1] = np.arange(128) + 1
    pc[:, PC_RIDX1] = 128 - np.arange(128)
    pr = np.stack([f(inp["rwkv_ln_g"][l]), f(inp["rwkv_ln_b"][l]), f(inp["gla_norm_g"][l]),
                   f(inp["ret_norm_g"][l]), f(inp["norm_post_mix"][l]), f(inp["norm_post_mlp"][l])], 0)
    cT = np.zeros((128, 16), np.float32)
    cc = col(inp["c_ctx"])
    cl = col(inp["c"][b])
    cT[:, 0::2] = cc
    cT[:, 1::2] = cl
    w2s = np.concatenate([f(inp["rwkv_w2"][l][0]), f(inp["rwkv_w2"][l][1])], 0)
    ga2 = np.zeros((64, 512), np.float32)
    ga2[0:16] = inp["gla_a2"][l][0]
    ga2[32:48] = inp["gla_a2"][l][1]
    gab = np.concatenate([f(inp["gla_a_b"][l][0]), f(inp["gla_a_b"][l][1])])[None, :]
    if l > 0:
        v1 = f(inp["rwkv_v1"][l - 1])
        v2 = f(inp["rwkv_v2"][l - 1])
    else:
        v1 = np.zeros((D, 32), np.float32)
        v2 = np.zeros((32, D), np.float32)
    return {
        "cT": cT, "ada_w": f(inp["ada_w"][l]), "ada_b": f(inp["ada_b"][l])[None, :],
        "w_in": w_in, "w_lora": w_lora, "w_ad": w_ad, "pcol": pc, "prow": np.ascontiguousarray(pr),
        "w2s": np.ascontiguousarray(w2s), "a2": f(inp["rwkv_a2"][l]), "g2": f(inp["rwkv_g2"][l]),
        "v1": v1, "v2": v2, "gla_a2": ga2, "gla_ab": np.ascontiguousarray(gab),
        "w_branch": f(inp["w_branch"][l]), "w_out": f(inp["w_out"][l]),
        "mlp_w1": f(inp["mlp_w1"][l]), "mlp_w2": f(inp["mlp_w2"][l]),
    }


LAYER_IN_SHAPES = {
    "cT": [128, 16], "ada_w": [D, 6 * D], "ada_b": [1, 6 * D], "w_in": [D, 13664], "w_lora": [D, 384],
    "w_ad": [D, 64], "pcol": [128, NPC], "prow": [NPR, D], "w2s": [128, D], "a2": [64, D], "g2": [128, D],
    "v1": [D, 32], "v2": [32, D], "gla_a2": [64, 512], "gla_ab": [1, 1024], "w_branch": [3, D, D],
    "w_out": [D, D], "mlp_w1": [D, 4 * D], "mlp_w2": [4 * D, D],
}


class Rot:
    def __init__(self, items):
        self.items = list(items)
        self.i = 0

    def __call__(self):
        x = self.items[self.i % len(self.items)]
        self.i += 1
        return x


def load_cast(P, dst, src_dt, key, rows, c0, ncols, stage_rot, engs=("dve", "pool"), d0=0, r0=0):
    nkb = rows // 128
    i = 0
    for kb in range(nkb):
        for cc in range(0, ncols, 1024):
            n = min(1024, ncols - cc)
            stg = stage_rot()
            P.dma(stg[:, 0:n], src_dt.v(key, src_dt.ap[r0 + kb * 128:r0 + (kb + 1) * 128, c0 + cc:c0 + cc + n]),
                  q="sp" if i % 2 == 0 else "pool")
            P.cp(dst[:, kb, d0 + cc:d0 + cc + n], stg[:, 0:n], eng=engs[i % len(engs)])
            i += 1


class LayerCtx:
    pass


def emit_layer(P, cfg, io, dbg=None):
    NT, NTL, NCT = cfg.NT, cfg.NTL, cfg.NCT
    L = LayerCtx()
    L.P, L.cfg, L.io, L.dbg = P, cfg, io, dbg
    with ExitStack() as st0:
        P.stack = st0
        L.pb = [P.ps("pb%d" % i, [128, 512]) for i in range(8)]
        pcol = L.pcol = P.sb("pcol", [128, NPC])
        P.dma(pcol[:], io["pcol"].v(0, io["pcol"].ap[:, :]))
        cmat = L.cmat = P.sb("cmat", [128, 8, 128])
        P.dma(cmat[:], io["cmat"].v(0, io["cmat"].ap.rearrange("p (a b) -> p a b", a=8)))
        L.ident = cmat[:, CM_ID, :]
        ones = L.ones = P.sb("ones", [128, 128])
        P.memset(ones[:], 1.0)
        L.eps6 = P.sb("eps6", [128, 1]); P.memset(L.eps6[:], 1e-6)
        L.eps5 = P.sb("eps5", [128, 1]); P.memset(L.eps5[:], 1e-5)
        L.epsg = P.sb("epsg", [128, 1]); P.memset(L.epsg[:], 64e-5)
        dc = L.dc = P.sb("dcol", [128, 8, 27])
        mu = pcol[:, PC_MU:PC_MU + 27]
        P.ts(dc[:, 0, :], mu, -1.0, ALU.mult, 1.0, ALU.add)
        for j in range(4):
            P.ts(dc[:, 1 + j, :], mu, pcol[:, PC_SLOT + j:PC_SLOT + j + 1], ALU.mult)
        P.tt(dc[:, 5, :], dc[:, 1, :], dc[:, 3, :], ALU.add)
        P.tt(dc[:, 6, :], dc[:, 2, :], dc[:, 4, :], ALU.add)
        L.omka = P.sb("omka", [128, 8])
        P.ts(L.omka[:], pcol[:, PC_KA:PC_KA + 8], -1.0, ALU.mult, 1.0, ALU.add)
        L.modcol = P.sb("modcol", [128, 48, 2])
        L.G1 = P.sb("G1", [128, 8, 2]); L.S1 = P.sb("S1", [128, 8, 2])
        L.G2 = P.sb("G2", [128, 8, 2]); L.S2 = P.sb("S2", [128, 8, 2])
        emit_mod(L)
        P.barrier()
        stages = {"rwkv": ["rw0", "rw1"], "only_gla": ["gl0", "gl1"], "only_ret": ["rt0", "rt1"], "only_post": ["post"],
                  "only_gla0": ["gl0"], "only_ret0": ["rt0"]}.get(dbg, ["rw0", "rw1", "gl0", "gl1", "rt0", "rt1", "post"])
        for sname in stages:
            if sname[:2] == "rw":
                emit_rwkv(L, int(sname[2]))
            elif sname[:2] == "gl":
                emit_gla(L, int(sname[2]))
            elif sname[:2] == "rt":
                emit_ret(L, int(sname[2]))
            else:
                emit_post(L)
            P.barrier()
    P.stack = None


def emit_mod(L):
    P, io = L.P, L.io
    with ExitStack() as st:
        P.stack = st
        pb = L.pb
        cT = P.sb("cT", [128, 16]); P.dma(cT[:], io["cT"].v(0, io["cT"].ap[:, :]))
        sc = P.sb("sc", [128, 16]); P.act(sc[:], cT[:], AF.Silu)
        adab = P.sb("adab", [2, 6 * D])
        P.dma(adab[:], io["ada_b"].v(0, io["ada_b"].ap[0, :].partition_broadcast(2)))
        modrow = P.sb("modrow", [2, 6 * D])
        stg = Rot([P.sb("adstg%d" % i, [128, 8, 512]) for i in range(2)])
        for cg in range(12):
            s = stg()
            for kb in range(8):
                P.dma(s[:, kb, :], io["ada_w"].v(0, io["ada_w"].ap[kb * 128:(kb + 1) * 128, cg * 512:(cg + 1) * 512]),
                      q="sp" if kb % 2 == 0 else "pool")
            ps = pb[cg % 2]
            for kb in range(8):
                P.mm(ps[0:2, :], sc[:, kb * 2:kb * 2 + 2], s[:, kb, :], start=(kb == 0), stop=(kb == 7))
            P.tt(modrow[:, cg * 512:(cg + 1) * 512], ps[0:2, :], adab[:, cg * 512:(cg + 1) * 512], ALU.add)
        pt = pb[2]
        for blk in range(48):
            P.tr(pt[:, blk * 2:blk * 2 + 2], modrow[0:2, blk * 128:(blk + 1) * 128], L.cmat[0:2, CM_ID, 0:2])
        P.cp(L.modcol.v(L.modcol.t[:].rearrange("p a b -> p (a b)")), pt[:, 0:96])
        mc = L.modcol
        pcol = L.pcol
        for j in range(2):
            P.stt(L.G1[:, :, j], mc[:, 8:16, j], 1.0, pcol[:, PC_NPM:PC_NPM + 8], ALU.add, ALU.mult)
            P.cp(L.S1[:, :, j], mc[:, 0:8, j])
            P.stt(L.G2[:, :, j], mc[:, 32:40, j], 1.0, pcol[:, PC_NPL:PC_NPL + 8], ALU.add, ALU.mult)
            P.cp(L.S2[:, :, j], mc[:, 24:32, j])
        P.dma(io["modrow"].v(0, io["modrow"].ap[:, :]), modrow[:])
        P.barrier()
    P.stack = None


def build_hT(L, dst, ti, G, S, src_dt, bufs):
    P = L.P
    j = 0 if ti < L.cfg.NCT else 1
    xt, xn, ss = bufs["xt"](), bufs["xn"](), bufs["ss"]()
    junk = xn
    P.dma(xt[:], src_dt.v(ti, src_dt.ap[ti * 128:(ti + 1) * 128, :]))
    P.act(junk[:], xt[:], AF.Square, accum=ss[:, 0:1])
    P.act(ss[:, 1:2], ss[:, 0:1], AF.Sqrt, scale=1.0 / D, bias=L.eps6[:])
    P.recip(ss[:, 2:3], ss[:, 1:2])
    P.ts(xn[:], xt[:], ss[:, 2:3], ALU.mult, eng="pool")
    for half in range(2):
        ps = bufs["ps"]()
        for q in range(4):
            kb = half * 4 + q
            P.tr(ps[:, q * 128:(q + 1) * 128], xn[:, kb * 128:(kb + 1) * 128], L.ident)
        for q in range(4):
            kb = half * 4 + q
            o = V(dst.ap[:, kb, :], dst.res)
            if q % 2 == 0:
                P.act(o, ps[:, q * 128:(q + 1) * 128], AF.Identity, scale=G[:, kb, j:j + 1], bias=S[:, kb, j:j + 1])
            else:
                P.ts(o, ps[:, q * 128:(q + 1) * 128], G[:, kb, j:j + 1], ALU.mult, S[:, kb, j:j + 1], ALU.add)
    return xt


def emit_rwkv(L, d):
    P, cfg, io, pb, pcol, cmat = L.P, L.cfg, L.io, L.pb, L.pcol, L.cmat
    NCT, NTL, NG = cfg.NCT, cfg.NTL, cfg.NG
    ident = L.ident
    with ExitStack() as st:
        P.stack = st
        Wr = P.sb("Wr", [128, 8, 3456], BF16)
        with ExitStack() as st2:
            P.stack = st2
            stg = Rot([P.sb("wstg%d" % i, [128, 1024]) for i in range(2)])
            load_cast(P, Wr, io["w_in"], 0, D, 0, 3072, stg)
            load_cast(P, Wr, io["w_lora"], 0, D, 0, 384, stg, d0=3072)
            P.barrier()
        P.stack = st
        w2s = P.sb("w2s", [128, D]); P.memset(w2s[:], 0.0)
        P.dma(w2s[d * 64:(d + 1) * 64, :], io["w2s"].v(0, io["w2s"].ap[d * 64:(d + 1) * 64, :]))
        a2 = P.sb("a2", [64, D]); P.dma(a2[:], io["a2"].v(0, io["a2"].ap[:, :]))
        g2 = P.sb("g2", [128, D]); P.dma(g2[:], io["g2"].v(0, io["g2"].ap[:, :]))
        v1 = P.sb("v1", [128, 8, 32]); P.dma(v1[:], io["v1"].v(0, io["v1"].ap.rearrange("(kb p) c -> p kb c", p=128)))
        v2 = P.sb("v2", [32, D]); P.dma(v2[:], io["v2"].v(0, io["v2"].ap[:, :]))
        cS, cI = (CM_SF, CM_IF) if d == 0 else (CM_SB, CM_IB)
        cX = CM_SB if d == 0 else CM_SF
        maskMA = P.sb("maskMA", [128, 2, 2, 128]); maskNB = P.sb("maskNB", [128, 2, 2, 128])
        maskX = P.sb("maskX", [128, 2, 128])
        for h in range(2):
            P.cp(maskMA[:, h, 0, :], cmat[:, cS, :]); P.cp(maskMA[:, h, 1, :], cmat[:, cI, :])
            P.ts(maskNB[:, h, 0, :], cmat[:, cS, :], -1.0, ALU.mult); P.cp(maskNB[:, h, 1, :], cmat[:, cI, :])
            P.ts(maskX[:, h, :], cmat[:, cX, :], -1.0, ALU.mult)
        Hbd = P.sb("Hbd", [128, 8, 128]); P.memset(Hbd[:], 0.0)
        GTbd = P.sb("GTbd", [128, 2, 128]); P.memset(GTbd[:], 0.0)
        dHbd = P.sb("dHbd", [128, 2, 128]); P.memset(dHbd[:], 0.0)
        hT4 = P.sb("hT4", [128, 8, 512], BF16)
        P.memset(hT4[:], 0.0)
        xt0 = P.sb("xt0", [128, D])
        bufs = {"xt": Rot([xt0]), "xn": Rot([P.sb("xn0", [128, D])]),
                "ss": Rot([P.sb("ss%d" % i, [128, 4]) for i in range(2)]),
                "ps": Rot([pb[6], pb[7]])}
        pp = P.sb("pp", [128, 27, 256])
        tw = P.sb("tw", [128, 256]); sgd = P.sb("sgd", [128, 256]); vv1 = P.sb("vv1", [32, 256])
        EW = []
        for i in range(2):
            e = {}
            for nm in ("a", "t1", "vp", "kap", "kpr", "b", "sg", "cs", "pre", "suf", "Ea", "Eb", "t2"):
                e[nm] = P.sb("ew%d_%s" % (i, nm), [128, 256])
            e["PC"] = P.sb("ew%d_PC" % i, [128, 2])
            e["khat"], e["bhat"], e["kbar"], e["bbar"] = e["sg"], e["cs"], e["a"], e["t2"]
            e["KR"] = P.sb("ew%d_KR" % i, [128, 2, 256])
            EW.append(e)
        MA = [P.sb("MA%d" % i, [128, 2, 2, 128]) for i in range(2)]
        NB = [P.sb("NB%d" % i, [128, 2, 2, 128]) for i in range(2)]
        Xa = [P.sb("Xa%d" % i, [128, 4, 128]) for i in range(2)]
        XTa = [P.sb("XTa%d" % i, [128, 4, 128]) for i in range(2)]
        Y = P.sb("Yut", [128, 4, 128])
        NU = P.sb("NUut", [128, 2, 4, 64])
        TM = [P.sb("TM%d" % i, [128, 3, 128]) for i in range(2)]
        QE = P.sb("QE", [128, 2, 128])
        o_sb = [P.sb("o_sb%d" % i, [128, D]) for i in range(2)]
        o_prev = xt0
        bon = P.sb("bon", [128, 8, 256]) if d == 0 else None

        groups = [("ctx", 0, [0, 1])] + [("lat", g, [NCT + 2 * g, NCT + 2 * g + 1]) for g in range(NG)]
        if d == 1:
            groups = [groups[0]] + groups[1:][::-1]
        G1, S1 = L.G1, L.S1
        om = lambda blk: L.dc[:, 0, blk:blk + 1]
        mus = lambda j, blk: L.dc[:, 1 + j, blk:blk + 1]
        for (kind, g, tiles) in groups:
            if kind == "ctx":
                for i, ti in enumerate(tiles):
                    build_hT(L, hT4[:, :, (1 + i) * 128:(2 + i) * 128], ti, G1, S1, io["xin"], bufs)
                w0, wn = 128, 256
            else:
                t0 = tiles[0]
                for i in range(4):
                    ti = t0 - 1 + i
                    if (g == 0 and i == 0) or (g == NG - 1 and i == 3):
                        continue
                    build_hT(L, hT4[:, :, i * 128:(i + 1) * 128], ti, G1, S1, io["xin"], bufs)
                w0, wn = 64, 384
            for blk in range(27):
                ps = pb[blk % 2]
                for kb in range(8):
                    P.mm(ps[:, 0:wn], Wr[:, kb, blk * 128:(blk + 1) * 128], hT4[:, kb, w0:w0 + wn],
                         start=(kb == 0), stop=(kb == 7))
                if kind == "ctx":
                    ov = pp[:, blk, :]
                    P.act(ov, ps[:, 0:256], AF.Identity, scale=om(blk))
                    P.stt(pp[:, blk, 1:256], ps[:, 0:255], L.dc[:, 5, blk:blk + 1], pp[:, blk, 1:256], ALU.mult, ALU.add)
                    P.stt(pp[:, blk, 0:255], ps[:, 1:256], L.dc[:, 6, blk:blk + 1], pp[:, blk, 0:255], ALU.mult, ALU.add)
                else:
                    pv = ps.t[:, 0:384].rearrange("p (r c) -> p r c", c=64)
                    ovt = pp.t[:, blk, :].rearrange("p (r c) -> p r c", c=64)
                    pvv = lambda ap: V(ap, ps.res)
                    ovv = lambda ap: V(ap, pp.res)
                    P.act(ovv(ovt), pvv(pv[:, 1:5, :]), AF.Identity, scale=om(blk))
                    P.stt(ovv(ovt[:, :, 1:64]), pvv(pv[:, 1:5, 0:63]), mus(0, blk), ovv(ovt[:, :, 1:64]), ALU.mult, ALU.add)
                    P.stt(ovv(ovt[:, :, 0:63]), pvv(pv[:, 1:5, 1:64]), mus(1, blk), ovv(ovt[:, :, 0:63]), ALU.mult, ALU.add)
                    ql = 1 if g == 0 else 0
                    P.stt(ovv(ovt[:, ql:4, :]), pvv(pv[:, ql:4, :]), mus(2, blk), ovv(ovt[:, ql:4, :]), ALU.mult, ALU.add)
                    qh = 3 if g == NG - 1 else 4
                    P.stt(ovv(ovt[:, 0:qh, :]), pvv(pv[:, 2:2 + qh, :]), mus(3, blk), ovv(ovt[:, 0:qh, :]), ALU.mult, ALU.add)
            tok0 = tiles[0] * 128
            P.act(tw[:], pp[:, 24, :], AF.Tanh)
            if d == 0:
                P.act(sgd[:], pp[:, 26, :], AF.Sigmoid)
            pv1 = pb[2]
            for cb in range(8):
                P.mm(pv1[0:32, 0:256], v1[:, cb, :], pp[:, 16 + cb, :], start=(cb == 0), stop=(cb == 7))
            P.cp(vv1[:], pv1[0:32, 0:256])
            chunks = [0, 1] if d == 0 else [1, 0]
            for quad in range(4):
                for i in range(2):
                    cb = 2 * quad + i
                    E = EW[i]
                    r_, k_, v_ = pp[:, cb, :], pp[:, 8 + cb, :], pp[:, 16 + cb, :]
                    pc = lambda base: pcol[:, base + cb:base + cb + 1]
                    pa = pb[2 + i]
                    P.mm(pa[:, 0:256], a2[:, cb * 128:(cb + 1) * 128], pp[0:64, 25, :])
                    P.act(E["a"][:], pa[:, 0:256], AF.Sigmoid, bias=pc(PC_A0))
                    P.mm(pa[:, 256:512], v2[:, cb * 128:(cb + 1) * 128], vv1[:])
                    P.act(E["t1"][:], pa[:, 256:512], AF.Sigmoid, bias=pc(PC_V0))
                    P.dma(E["t2"][:], io["vfin"].v((cb, tiles[0]), io["vfin"].ap[cb, :, tok0:tok0 + 256]))
                    P.tt(E["t2"][:], E["t2"][:], v_, ALU.subtract)
                    P.stt(E["t2"][:], E["t2"][:], pcol[:, PC_FLAG:PC_FLAG + 1], E["t1"][:], ALU.mult, ALU.mult)
                    P.tt(E["vp"][:], E["t2"][:], v_, ALU.add)
                    if d == 0:
                        P.dma(io["vout"].v((cb, tiles[0]), io["vout"].ap[cb, :, tok0:tok0 + 256]), E["vp"][:], q="pool")
                    P.ts(E["kap"][:], k_, pc(PC_KK), ALU.mult, eng="pool")
                    P.tt(E["t1"][:], E["kap"][:], E["kap"][:], ALU.mult, eng="pool")
                    pq = pb[4 + i]
                    P.mm(pq[:, 0:256], cmat[:, CM_BO, :], E["t1"][:])
                    P.ts(E["t1"][:], pq[:, 0:256], 1e-12, ALU.max)
                    P.act(E["t1"][:], E["t1"][:], AF.Sqrt)
                    P.recip(E["t1"][:], E["t1"][:])
                    P.tt(E["kap"][:], E["kap"][:], E["t1"][:], ALU.mult)
                    P.ts(E["t1"][:], E["a"][:], pc(PC_KA), ALU.mult, L.omka[:, cb:cb + 1], ALU.add)
                    P.tt(E["kpr"][:], k_, E["t1"][:], ALU.mult)
                    P.tt(E["b"][:], E["kap"][:], E["a"][:], ALU.mult, eng="pool")
                    if d == 0:
                        P.stt(E["t1"][:], r_, pc(PC_RK), E["kpr"][:], ALU.mult, ALU.mult)
                        P.mm(pq[:, 256:512], cmat[:, CM_BO, :], E["t1"][:])
                        P.tt(bon[:, cb, :], pq[:, 256:512], E["vp"][:], ALU.mult)
                    P.mm(pq[:, 0:256], w2s[:, cb * 128:(cb + 1) * 128], tw[:])
                    P.act(E["sg"][:], pq[:, 0:256], AF.Sigmoid, bias=pc(PC_W0F if d == 0 else PC_W0B))
                    for c in range(2):
                        cs_ = slice(c * 128, (c + 1) * 128)
                        P.scan(E["cs"][:, cs_], L.ones[:, 0:128], E["sg"][:, cs_], 0.0, ALU.mult, ALU.add)
                    P.tt(E["pre"][:], E["cs"][:], E["sg"][:], ALU.subtract, eng="pool")
                    for c in range(2):
                        cs_ = slice(c * 128, (c + 1) * 128)
                        P.ts(E["suf"][:, cs_], E["cs"][:, cs_], -1.0, ALU.mult,
                             E["cs"][:, c * 128 + 127:c * 128 + 128], ALU.add)
                    if d == 0:
                        incl, excl, lo = E["cs"], E["pre"], E["suf"]
                    else:
                        P.tt(E["t1"][:], E["suf"][:], E["sg"][:], ALU.add, eng="pool")
                        incl, excl, lo = E["t1"], E["suf"], E["pre"]
                    P.act(E["Ea"][:], incl[:], AF.Exp, scale=-ALPHA)
                    for c in range(2):
                        pcc = (c * 128 + 127) if d == 0 else (c * 128)
                        P.cp(E["PC"][:, c:c + 1], E["Ea"][:, pcc:pcc + 1], eng="pool")
                    P.tt(E["KR"][:, 1, :], r_, E["Ea"][:], ALU.mult, eng="pool")
                    P.act(E["Eb"][:], incl[:], AF.Exp, scale=ALPHA)
                    P.tt(E["khat"][:], E["kpr"][:], E["Eb"][:], ALU.mult)
                    P.tt(E["bhat"][:], E["b"][:], E["Eb"][:], ALU.mult, eng="pool")
                    P.act(E["Ea"][:], excl[:], AF.Exp, scale=-ALPHA)
                    P.tt(E["KR"][:, 0, :], E["kap"][:], E["Ea"][:], ALU.mult)
                    P.act(E["Eb"][:], lo[:], AF.Exp, scale=-ALPHA)
                    P.tt(E["kbar"][:], E["kpr"][:], E["Eb"][:], ALU.mult)
                    P.tt(E["bbar"][:], E["b"][:], E["Eb"][:], ALU.mult, eng="pool")
                for c in chunks:
                    cs_ = slice(c * 128, (c + 1) * 128)
                    pY = pb[2]
                    for i in range(2):
                        E = EW[i]
                        pt = pb[4 + i]
                        for qn, nm in enumerate(("vp", "kbar", "bbar")):
                            P.tr(pt[:, qn * 128:(qn + 1) * 128], E[nm][:, cs_], ident)
                        P.cp(TM[i].v(TM[i].t[:].rearrange("p a b -> p (a b)")), pt[:, 0:384], eng="act")
                        pA, pB, pC = pb[0], pb[1], pb[3]
                        for hh in range(2):
                            rs = slice(hh * 64, (hh + 1) * 64)
                            krr = V(E["KR"].t[rs, :, cs_], E["KR"].res)
                            P.mm(V(pA.t[:, hh * 256:(hh + 1) * 256].rearrange("p (a b) -> p a b", a=2), pA.res),
                                 E["khat"][rs, cs_], krr)
                            P.mm(V(pB.t[:, hh * 256:(hh + 1) * 256].rearrange("p (a b) -> p a b", a=2), pB.res),
                                 E["bhat"][rs, cs_], krr)
                            P.mm(pC[:, hh * 128:(hh + 1) * 128], E["KR"][rs, 0, cs_], E["bhat"][rs, cs_])
                        fl = lambda t: t.v(t.t[:].rearrange("p a b c -> p (a b c)"))
                        P.tt(fl(MA[i]), pA[:, :], fl(maskMA), ALU.mult)
                        P.tt(fl(NB[i]), pB[:, :], fl(maskNB), ALU.mult)
                        P.tt(Xa[0][:, 2 * i:2 * i + 2, :], V(pC.t[:, 0:256].rearrange("p (a b) -> p a b", a=2), pC.res),
                             maskX[:], ALU.mult)
                        for hh in range(2):
                            h = 2 * i + hh
                            rs = slice(hh * 64, (hh + 1) * 64)
                            P.cp(XTa[0][:, h, :], NB[i][:, hh, 0, :], eng="pool")
                            P.tr(pY[:, h * 128:h * 128 + 64], E["KR"][rs, 0, cs_], ident[rs, rs] if False else V(cmat.t[rs, CM_ID, hh * 64:(hh + 1) * 64], cmat.res))
                            P.mm(pY[:, h * 128 + 64:h * 128 + 128], MA[i][:, hh, 0, :], TM[i][:, 0, hh * 64:(hh + 1) * 64])
                    Yf = Y.v(Y.t[:].rearrange("p a b -> p (a b)"))
                    P.cp(Yf, pY[:, :])
                    cur = 0
                    for lev in range(7):
                        pAp = pb[0]
                        for h in range(4):
                            P.mm(pAp[:, h * 128:(h + 1) * 128], XTa[cur][:, h, :], Y[:, h, :])
                        if lev < 6:
                            p1, p2 = pb[1], pb[3]
                            for h in range(4):
                                P.mm(p1[:, h * 128:(h + 1) * 128], XTa[cur][:, h, :], Xa[cur][:, h, :])
                                P.mm(p2[:, h * 128:(h + 1) * 128], Xa[cur][:, h, :], XTa[cur][:, h, :])
                            nx = 1 - cur
                            P.cp(Xa[nx].v(Xa[nx].t[:].rearrange("p a b -> p (a b)")), p1[:, :], eng="act")
                            P.cp(XTa[nx].v(XTa[nx].t[:].rearrange("p a b -> p (a b)")), p2[:, :], eng="dve")
                            P.tt(Yf, pAp[:, :], Yf, ALU.add)
                            cur = nx
                        else:
                            for w in range(2):
                                P.stt(NU[:, w, :, :], V(pAp.t[:, :].rearrange("p (h w j) -> p h w j", h=4, w=2)[:, :, w, :], pAp.res),
                                      -1.0, V(Y.t[:, :, w * 64:(w + 1) * 64], Y.res), ALU.mult, ALU.subtract)
                    for i in range(2):
                        cb = 2 * quad + i
                        E = EW[i]
                        pQ, pO, pG, pD = pb[4], pb[5], pb[6], pb[7]
                        nUk2 = V(NU.t[:, 0, 2 * i:2 * i + 2, :].rearrange("p a b -> p (a b)"), NU.res)
                        nU02 = NU[:, 1, 2 * i:2 * i + 2, :]
                        P.mm(V(pQ.t[:, 0:256].rearrange("p (a b) -> p a b", a=2), pQ.res), nUk2, NB[i][:, :, 1, :])
                        for hh in range(2):
                            h = 2 * i + hh
                            rs = slice(hh * 64, (hh + 1) * 64)
                            P.mm(pO[:, i * 128 + hh * 64:i * 128 + (hh + 1) * 64], MA[i][:, hh, 1, :], TM[i][:, 0, rs], start=True, stop=False)
                            P.mm(pO[:, i * 128 + hh * 64:i * 128 + (hh + 1) * 64], NB[i][:, hh, 1, :], NU[:, 1, h, :], start=False, stop=True)
                        P.mm(pG[:, 0:128], nUk2, TM[i][:, 2, :])
                        P.mm(V(pD.t[:, 0:128].rearrange("p (a b) -> p a b", a=2), pD.res), TM[i][:, 1, :], V(TM[i].t[:, 0, :].rearrange("p (a b) -> p a b", a=2), TM[i].res), start=True, stop=False)
                        P.mm(V(pD.t[:, 0:128].rearrange("p (a b) -> p a b", a=2), pD.res), TM[i][:, 2, :], nU02, start=False, stop=True)
                        for hh in range(2):
                            rs = slice(hh * 64, (hh + 1) * 64)
                            fs = slice(hh * 64, (hh + 1) * 64)
                            P.tt(QE[rs, i, :], pQ[rs, hh * 128:(hh + 1) * 128], E["KR"][rs, 1, cs_], ALU.add)
                            pcol_pc = E["PC"][rs, c:c + 1]
                            P.stt(GTbd[rs, i, fs], V(cmat.t[rs, CM_ID, fs], cmat.res), pcol_pc, pG[rs, fs], ALU.mult, ALU.add)
                            P.cp(dHbd[rs, i, fs], pD[rs, fs], eng="act")
                        pS, pH = pb[0], pb[1]
                        P.mm(pS[:, i * 128:(i + 1) * 128], QE[:, i, :], Hbd[:, cb, :])
                        P.mm(pH[:, i * 128:(i + 1) * 128], GTbd[:, i, :], Hbd[:, cb, :])
                        P.cp(o_sb[c][:, cb * 128:(cb + 1) * 128], pO[:, i * 128:(i + 1) * 128], eng="act")
                        P.tt(o_sb[c][:, cb * 128:(cb + 1) * 128], pS[:, i * 128:(i + 1) * 128],
                             o_sb[c][:, cb * 128:(cb + 1) * 128], ALU.add)
                        P.tt(Hbd[:, cb, :], pH[:, i * 128:(i + 1) * 128], dHbd[:, i, :], ALU.add)
            for c in range(2):
                ti = tiles[c]
                rows = slice(ti * 128, (ti + 1) * 128)
                if d == 1:
                    P.dma(o_prev[:], io["orw"].v(ti, io["orw"].ap[rows, :]))
                    P.tt(o_sb[c][:], o_sb[c][:], o_prev[:], ALU.add, eng="pool")
                P.dma(io["orw"].v(ti, io["orw"].ap[rows, :]), o_sb[c][:], q="pool")
                if d == 0:
                    cs_ = slice(c * 128, (c + 1) * 128)
                    for hf in range(2):
                        pg = pb[2 + hf]
                        P.mm(pg[:, :], sgd[:, cs_], g2[:, hf * 512:(hf + 1) * 512])
                        P.cp(o_prev[:, hf * 512:(hf + 1) * 512], pg[:, :], eng="act" if hf == 0 else "dve")
                    P.dma(io["gtok"].v(ti, io["gtok"].ap[rows, :]), o_prev[:], q="pool")
                    for hf in range(2):
                        pg = pb[4 + hf]
                        for q4 in range(4):
                            cb = hf * 4 + q4
                            P.tr(pg[:, q4 * 128:(q4 + 1) * 128], bon[:, cb, cs_], ident)
                        P.cp(o_prev[:, hf * 512:(hf + 1) * 512], pg[:, :], eng="act" if hf == 0 else "dve")
                    P.dma(io["bonus"].v(ti, io["bonus"].ap[rows, :]), o_prev[:], q="pool")
        P.barrier()
    P.stack = None


SCRATCH = {"modrow": lambda c: [2, 6 * D], "orw": lambda c: [c.NT, D], "gtok": lambda c: [c.NT, D],
           "bonus": lambda c: [c.NT, D], "ogl": lambda c: [c.NT, D], "ort": lambda c: [c.NT, D],
           "mrg": lambda c: [c.NT, D], "xmid": lambda c: [c.NT, D]}


def build_layer_program(cfg, dbg=None):
    nc = bass.Bass("TRN2", target_bir_lowering=False)
    io = {}

    def din(name, shape):
        io[name] = DT(nc.dram_tensor(name, list(shape), F32, kind="ExternalInput").ap(), name)

    for k, shp in LAYER_IN_SHAPES.items():
        din(k, shp)
    din("xin", [cfg.NT, D])
    din("vfin", [8, 128, cfg.NT])
    din("cmat", [128, 8 * 128])
    din("cosT", [128, cfg.NT])
    din("sinT", [128, cfg.NT])
    din("cosK", [cfg.NT, 128])
    din("sinK", [cfg.NT, 128])
    for k in ("xout",):
        io[k] = DT(nc.dram_tensor(k, [cfg.NT, D], F32, kind="ExternalOutput").ap(), k)
    io["vout"] = DT(nc.dram_tensor("vout", [8, 128, cfg.NT], F32, kind="ExternalOutput").ap(), "vout")
    for k, f in SCRATCH.items():
        kind = "ExternalOutput" if dbg else "Internal"
        io[k] = DT(nc.dram_tensor(k, f(cfg), F32, kind=kind).ap(), k)
    P = Prog(nc)
    emit_layer(P, cfg, io, dbg=dbg)
    P.finish()
    return nc, P


def tile_order(cfg, d):
    ctx = list(range(cfg.NCT))
    lat = list(range(cfg.NCT, cfg.NTL))
    return (ctx + lat) if d == 0 else (ctx[::-1] + lat[::-1])


def emit_gla(L, d):
    P, cfg, io, pb, pcol, cmat = L.P, L.cfg, L.io, L.pb, L.pcol, L.cmat
    O_GLA = 3392
    with ExitStack() as st:
        P.stack = st
        Wg = P.sb("Wg", [128, 8, 2112], BF16)
        with ExitStack() as st2:
            P.stack = st2
            stg = Rot([P.sb("wstg%d" % i, [128, 1024]) for i in range(2)])
            load_cast(P, Wg, io["w_in"], 0, D, O_GLA, 2048, stg)
            load_cast(P, Wg, io["w_ad"], 0, D, 0, 64, stg, d0=2048)
            P.barrier()
        P.stack = st
        ga2 = P.sb("ga2", [64, 512]); P.memset(ga2[:], 0.0)
        P.dma(ga2[d * 32:d * 32 + 16, :], io["gla_a2"].v(0, io["gla_a2"].ap[d * 32:d * 32 + 16, :]))
        gab = P.sb("gab", [1, 512]); P.dma(gab[:], io["gla_ab"].v(0, io["gla_ab"].ap[0:1, d * 512:(d + 1) * 512]))
        cI = CM_IF if d == 0 else CM_IB
        cK = CM_SB if d == 0 else CM_SF
        mask4 = P.sb("mask4", [128, 4, 128])
        for h in range(4):
            P.cp(mask4[:, h, :], cmat[:, cI, :])
        S = P.sb("Sg", [128, 4, 256]); P.memset(S[:], 0.0)
        Sbf = P.sb("Sgbf", [128, 4, 256], BF16); P.memset(Sbf[:], 0.0)
        hT1 = P.sb("hT1", [128, 8, 128], BF16)
        bufs = {"xt": Rot([P.sb("xt0", [128, D])]), "xn": Rot([P.sb("xn0", [128, D])]),
                "ss": Rot([P.sb("ss%d" % i, [128, 4]) for i in range(2)]), "ps": Rot([pb[6], pb[7]])}
        adT = P.sb("adT", [64, 128])
        v_sb = P.sb("v_sb", [128, D], BF16)
        e1 = P.sb("e1", [128, 512]); sp = P.sb("sp", [128, 512])
        Eq = P.sb("Eq", [128, 4, 128]); Ek = P.sb("Ek", [128, 4, 128])
        qin = P.sb("qin", [128, 4, 128], BF16); kin = P.sb("kin", [128, 4, 128], BF16)
        koe = P.sb("koe", [128, 512]); kout = P.sb("kout", [128, 512], BF16)
        sc_sb = P.sb("sc_sb", [128, 4, 128], BF16)
        o_sb = P.sb("o_sb", [128, D]); o_prev = P.sb("o_prev", [128, D])
        for ti in tile_order(cfg, d):
            rows = slice(ti * 128, (ti + 1) * 128)
            build_hT(L, hT1[:, :, :], ti, L.G1, L.S1, io["xin"], bufs)
            pq_, pk_ = pb[0], pb[1]
            for blk in range(8):
                dst = pq_ if blk < 4 else pk_
                for kb in range(8):
                    P.mm(dst[:, (blk % 4) * 128:(blk % 4 + 1) * 128], Wg[:, kb, blk * 128:(blk + 1) * 128], hT1[:, kb, :],
                         start=(kb == 0), stop=(kb == 7))
            pa = pb[2]
            for kb in range(8):
                P.mm(pa[0:64, 0:128], Wg[:, kb, 2048:2112], hT1[:, kb, :], start=(kb == 0), stop=(kb == 7))
            P.cp(adT[:], pa[0:64, 0:128], eng="act")
            for hf in range(2):
                pv = pb[3 + hf]
                for kb in range(8):
                    P.mm(pv[:, :], hT1[:, kb, :], Wg[:, kb, 1024 + hf * 512:1024 + (hf + 1) * 512], start=(kb == 0), stop=(kb == 7))
                P.cp(v_sb[:, hf * 512:(hf + 1) * 512], pv[:, :], eng="act" if hf == 0 else "dve")
            pg = pb[2]
            P.mm(pg[:, :], adT[:, :], ga2[:, :], start=True, stop=False)
            P.mm(pg[:, :], L.ones[0:1, 0:128], gab[:, :], start=False, stop=True)
            P.act(e1[:], pg[:, :], AF.Exp, scale=-1.0)
            P.act(sp[:], e1[:], AF.Ln, bias=L.ones[:, 0:1])
            pc_ = pb[3]
            for h in range(4):
                P.mm(pc_[:, h * 128:(h + 1) * 128], sp[:, h * 128:(h + 1) * 128], cmat[:, cI, :])
            fl = lambda t: t.v(t.t[:].rearrange("p a b -> p (a b)"))
            P.act(fl(Eq), pc_[:, :], AF.Exp, scale=-1.0 / 16.0)
            P.act(fl(Ek), pc_[:, :], AF.Exp, scale=1.0 / 16.0)
            P.stt(fl(qin), pq_[:, :], float(128 ** -0.5), fl(Eq), ALU.mult, ALU.mult)
            P.tt(fl(kin), pk_[:, :], fl(Ek), ALU.mult)
            pko = pb[4]
            P.mm(pko[:, :], cmat[:, cK, :], sp[:, :])
            P.act(koe[:], pko[:, :], AF.Exp, scale=-1.0 / 16.0)
            pkt = pb[5]
            for kb in range(8):
                P.mm(pkt[:, :], hT1[:, kb, :], Wg[:, kb, 512:1024], start=(kb == 0), stop=(kb == 7))
            P.tt(kout[:], pkt[:, :], koe[:], ALU.mult)
            psc = pb[0]
            for h in range(4):
                P.mm(psc[:, h * 128:(h + 1) * 128], kin[:, h, :], qin[:, h, :])
            P.tt(fl(sc_sb), psc[:, :], fl(mask4), ALU.mult)
            for hp in range(2):
                po = pb[1 + hp]
                for hh in range(2):
                    h = hp * 2 + hh
                    P.mm(po[:, hh * 256:(hh + 1) * 256], sc_sb[:, h, :], v_sb[:, h * 256:(h + 1) * 256], start=True, stop=False)
                    P.mm(po[:, hh * 256:(hh + 1) * 256], qin[:, h, :], Sbf[:, h, :], start=False, stop=True)
                P.cp(o_sb[:, hp * 512:(hp + 1) * 512], po[:, :], eng="act" if hp == 0 else "dve")
            for hp in range(2):
                pkv = pb[3 + hp]
                for hh in range(2):
                    h = hp * 2 + hh
                    P.mm(pkv[:, hh * 256:(hh + 1) * 256], kout[:, h * 128:(h + 1) * 128], v_sb[:, h * 256:(h + 1) * 256])
                for hh in range(2):
                    h = hp * 2 + hh
                    dcol = Eq[:, h, 127:128] if d == 0 else Eq[:, h, 0:1]
                    P.stt(S[:, h, :], S[:, h, :], dcol, pkv[:, hh * 256:(hh + 1) * 256], ALU.mult, ALU.add)
            P.cp(fl(Sbf), fl(S), eng="pool")
            if d == 1:
                P.dma(o_prev[:], io["ogl"].v(ti, io["ogl"].ap[rows, :]))
                P.tt(o_sb[:], o_sb[:], o_prev[:], ALU.add, eng="pool")
            P.dma(io["ogl"].v(ti, io["ogl"].ap[rows, :]), o_sb[:], q="pool")
        P.barrier()
    P.stack = None


def emit_ret(L, d):
    P, cfg, io, pb, pcol, cmat = L.P, L.cfg, L.io, L.pb, L.pcol, L.cmat
    O_RET = 6496
    KS = float(256 ** -0.5)
    with ExitStack() as st:
        P.stack = st
        Wt = P.sb("Wt", [128, 8, 3072], BF16)
        with ExitStack() as st2:
            P.stack = st2
            stg = Rot([P.sb("wstg%d" % i, [128, 1024]) for i in range(2)])
            load_cast(P, Wt, io["w_in"], 0, D, O_RET, 3072, stg)
            P.barrier()
        P.stack = st
        fl = lambda t: t.v(t.t[:].rearrange("p a b -> p (a b)"))
        lgc = P.sb("lgc", [128, 4])
        P.act(lgc[:], pcol[:, PC_RETD + d * 4:PC_RETD + d * 4 + 4], AF.Exp)
        P.ts(lgc[:], lgc[:], -1.0, ALU.mult)
        cD, cI = (CM_DF, CM_IF) if d == 0 else (CM_DB, CM_IB)
        Dm = P.sb("Dm", [128, 4, 128])
        qs = P.sb("qs", [128, 4]); ks = P.sb("ks", [128, 4]); dS = P.sb("dS", [128, 4])
        c128 = P.sb("c128", [128, 1]); P.memset(c128[:], 128.0)
        for h in range(4):
            P.act(Dm[:, h, :], cmat[:, cD, :], AF.Exp, scale=lgc[:, h:h + 1])
            P.tt(Dm[:, h, :], Dm[:, h, :], cmat[:, cI, :], ALU.mult)
            P.act(qs[:, h:h + 1], pcol[:, (PC_IDX1 if d == 0 else PC_RIDX1):(PC_IDX1 if d == 0 else PC_RIDX1) + 1], AF.Exp, scale=lgc[:, h:h + 1])
            P.act(ks[:, h:h + 1], pcol[:, (PC_RIDX if d == 0 else PC_IDX):(PC_RIDX if d == 0 else PC_IDX) + 1], AF.Exp, scale=lgc[:, h:h + 1])
            P.act(dS[:, h:h + 1], c128[:], AF.Exp, scale=lgc[:, h:h + 1])
        S = P.sb("Sr", [128, 8, 256]); P.memset(S[:], 0.0)
        Sbf = P.sb("Srbf", [128, 8, 256], BF16); P.memset(Sbf[:], 0.0)
        hT1 = P.sb("hT1", [128, 8, 128], BF16)
        bufs = {"xt": Rot([P.sb("xt0", [128, D])]), "xn": Rot([P.sb("xn0", [128, D])]),
                "ss": Rot([P.sb("ss%d" % i, [128, 4]) for i in range(2)]), "ps": Rot([pb[6], pb[7]])}
        cosf = P.sb("cosf", [128, 128]); sinf = P.sb("sinf", [128, 128])
        cosk = P.sb("cosk", [128, 128]); sink = P.sb("sink", [128, 128])
        cost = P.sb("cost", [128, 128]); sint = P.sb("sint", [128, 128])
        qr = P.sb("qr", [128, 8, 128], BF16); kr = P.sb("kr", [128, 8, 128], BF16)
        tA = P.sb("tA", [128, 4, 128]); tB = P.sb("tB", [128, 4, 128])
        v_sb = P.sb("v_sb", [128, D], BF16)
        ktk = P.sb("ktk", [128, 4, 2, 128]); kout = P.sb("kout", [128, D], BF16)
        sc_sb = P.sb("sc_sb", [128, 4, 128], BF16)
        o_sb = P.sb("o_sb", [128, D]); o_prev = P.sb("o_prev", [128, D])

        def bc(t, n):
            return V(t.t[:, :].unsqueeze(1).to_broadcast([128, n, 128]), t.res)

        for ti in tile_order(cfg, d):
            rows = slice(ti * 128, (ti + 1) * 128)
            build_hT(L, hT1[:, :, :], ti, L.G1, L.S1, io["xin"], bufs)
            P.dma(cosf[:], io["cosT"].v(0, io["cosT"].ap[:, rows]))
            P.dma(sinf[:], io["sinT"].v(0, io["sinT"].ap[:, rows]), q="pool")
            P.dma(cost[:], io["cosK"].v(0, io["cosK"].ap[rows, :]))
            P.dma(sint[:], io["sinK"].v(0, io["sinK"].ap[rows, :]), q="pool")
            P.ts(cosk[:], cosf[:], KS, ALU.mult, eng="pool")
            P.ts(sink[:], sinf[:], KS, ALU.mult, eng="pool")
            for which in range(2):
                dstT = qr if which == 0 else kr
                cc, sn = (cosf, sinf) if which == 0 else (cosk, sink)
                for bk in range(2):
                    ps = pb[which * 2 + bk]
                    for q4 in range(4):
                        blk = bk * 4 + q4
                        col0 = which * 1024 + blk * 128
                        for kb in range(8):
                            P.mm(ps[:, q4 * 128:(q4 + 1) * 128], Wt[:, kb, col0:col0 + 128], hT1[:, kb, :],
                                 start=(kb == 0), stop=(kb == 7))
                    pv = ps.t[:, :].rearrange("p (h w t) -> p h w t", h=2, w=2)
                    t1 = V(pv[:, :, 0, :], ps.res); t2 = V(pv[:, :, 1, :], ps.res)
                    dv = dstT.t[:, bk * 4:(bk + 1) * 4, :].rearrange("p (h w) t -> p h w t", w=2)
                    a_ = V(tA.t[:, 0:2, :], tA.res); b_ = V(tA.t[:, 2:4, :], tA.res)
                    c_ = V(tB.t[:, 0:2, :], tB.res); d_ = V(tB.t[:, 2:4, :], tB.res)
                    P.tt(a_, t1, bc(cc, 2), ALU.mult)
                    P.tt(b_, t2, bc(sn, 2), ALU.mult)
                    P.tt(V(dv[:, :, 0, :], dstT.res), a_, b_, ALU.subtract, eng="pool")
                    P.tt(c_, t1, bc(sn, 2), ALU.mult)
                    P.tt(d_, t2, bc(cc, 2), ALU.mult)
                    P.tt(V(dv[:, :, 1, :], dstT.res), c_, d_, ALU.add, eng="pool")
            for hf in range(2):
                pv_ = pb[4 + hf]
                for kb in range(8):
                    P.mm(pv_[:, :], hT1[:, kb, :], Wt[:, kb, 2048 + hf * 512:2048 + (hf + 1) * 512], start=(kb == 0), stop=(kb == 7))
                P.cp(v_sb[:, hf * 512:(hf + 1) * 512], pv_[:, :], eng="act" if hf == 0 else "dve")
            for hf in range(2):
                pk_ = pb[4 + hf]
                for kb in range(8):
                    P.mm(pk_[:, :], hT1[:, kb, :], Wt[:, kb, 1024 + hf * 512:1024 + (hf + 1) * 512], start=(kb == 0), stop=(kb == 7))
                pv = pk_.t[:, :].rearrange("p (h w t) -> p h w t", h=2, w=2)
                t1 = V(pv[:, :, 0, :], pk_.res); t2 = V(pv[:, :, 1, :], pk_.res)
                a_ = V(tA.t[:, 0:2, :], tA.res); b_ = V(tA.t[:, 2:4, :], tA.res)
                c_ = V(tB.t[:, 0:2, :], tB.res); d_ = V(tB.t[:, 2:4, :], tB.res)
                P.tt(a_, t1, bc(cost, 2), ALU.mult)
                P.tt(b_, t2, bc(sint, 2), ALU.mult)
                P.tt(ktk[:, hf * 2:hf * 2 + 2, 0, :], a_, b_, ALU.subtract, eng="pool")
                P.tt(c_, t1, bc(sint, 2), ALU.mult)
                P.tt(d_, t2, bc(cost, 2), ALU.mult)
                P.tt(ktk[:, hf * 2:hf * 2 + 2, 1, :], c_, d_, ALU.add, eng="pool")
            for h in range(4):
                P.ts(kout[:, h * 256:(h + 1) * 256], V(ktk.t[:, h, :, :].rearrange("p a b -> p (a b)"), ktk.res),
                     ks[:, h:h + 1], ALU.mult, KS, ALU.mult)
            psc = pb[0]
            for h in range(4):
                for w in range(2):
                    P.mm(psc[:, h * 128:(h + 1) * 128], kr[:, 2 * h + w, :], qr[:, 2 * h + w, :], start=(w == 0), stop=(w == 1))
            P.tt(fl(sc_sb), psc[:, :], fl(Dm), ALU.mult)
            for hp in range(2):
                pin, pit = pb[1], pb[2]
                for hh in range(2):
                    h = hp * 2 + hh
                    P.mm(pin[:, hh * 256:(hh + 1) * 256], sc_sb[:, h, :], v_sb[:, h * 256:(h + 1) * 256])
                    for w in range(2):
                        P.mm(pit[:, hh * 256:(hh + 1) * 256], qr[:, 2 * h + w, :], Sbf[:, 2 * h + w, :], start=(w == 0), stop=(w == 1))
                P.cp(o_sb[:, hp * 512:(hp + 1) * 512], pin[:, :], eng="act")
                for hh in range(2):
                    h = hp * 2 + hh
                    cs_ = slice(h * 256, (h + 1) * 256)
                    P.stt(o_sb[:, cs_], pit[:, hh * 256:(hh + 1) * 256], qs[:, h:h + 1], o_sb[:, cs_], ALU.mult, ALU.add)
            for blk2 in range(4):
                pkv = pb[3 + blk2 % 2]
                for w in range(2):
                    blk = blk2 * 2 + w
                    h = blk // 2
                    P.mm(pkv[:, w * 256:(w + 1) * 256], kout[:, blk * 128:(blk + 1) * 128], v_sb[:, h * 256:(h + 1) * 256])
                for w in range(2):
                    blk = blk2 * 2 + w
                    h = blk // 2
                    P.stt(S[:, blk, :], S[:, blk, :], dS[:, h:h + 1], pkv[:, w * 256:(w + 1) * 256], ALU.mult, ALU.add)
            P.cp(fl(Sbf), fl(S), eng="pool")
            if d == 1:
                P.dma(o_prev[:], io["ort"].v(ti, io["ort"].ap[rows, :]))
                P.tt(o_sb[:], o_sb[:], o_prev[:], ALU.add, eng="pool")
            P.dma(io["ort"].v(ti, io["ort"].ap[rows, :]), o_sb[:], q="pool")
        P.barrier()
    P.stack = None


def head_norm_tok(L, y, o, nh, eps_t, center, tmp, st):
    P = L.P
    hd = D // nh
    for h in range(nh):
        sl = slice(h * hd, (h + 1) * hd)
        if center:
            P.act(tmp[:, sl], o[:, sl], AF.Identity, accum=st[:, h:h + 1])
        P.act(tmp[:, sl], o[:, sl], AF.Square, accum=st[:, 16 + h:17 + h])
    if center:
        P.ts(st[:, 0:nh], st[:, 0:nh], 1.0 / hd, ALU.mult)
        P.tt(st[:, 32:32 + nh], st[:, 0:nh], st[:, 0:nh], ALU.mult)
        P.stt(st[:, 16:16 + nh], st[:, 16:16 + nh], 1.0 / hd, st[:, 32:32 + nh], ALU.mult, ALU.subtract)
        P.ts(st[:, 16:16 + nh], st[:, 16:16 + nh], 0.0, ALU.max)
        P.act(st[:, 16:16 + nh], st[:, 16:16 + nh], AF.Sqrt, bias=eps_t[:])
    else:
        P.act(st[:, 16:16 + nh], st[:, 16:16 + nh], AF.Sqrt, scale=1.0 / hd, bias=eps_t[:])
    P.recip(st[:, 32:32 + nh], st[:, 16:16 + nh])
    for h in range(nh):
        sl = slice(h * hd, (h + 1) * hd)
        if center:
            P.ts(y[:, sl], o[:, sl], st[:, h:h + 1], ALU.subtract, st[:, 32 + h:33 + h], ALU.mult,
                 eng="dve" if h % 2 == 0 else "pool")
        else:
            P.ts(y[:, sl], o[:, sl], st[:, 32 + h:33 + h], ALU.mult, eng="dve" if h % 2 == 0 else "pool")


def emit_post(L):
    P, cfg, io, pb, pcol, cmat = L.P, L.cfg, L.io, L.pb, L.pcol, L.cmat
    ident = L.ident
    O_GLA_R = 3392 + 2048
    O_RET_G = 6496 + 3072
    O_GATE = 10592
    with ExitStack() as st:
        P.stack = st
        Wga = P.sb("Wga", [128, 8, 3072], BF16)
        Wrg = P.sb("Wrg", [128, 8, 2048], BF16)
        Wb = P.sb("Wb", [128, 24, 1024], BF16)
        with ExitStack() as st2:
            P.stack = st2
            stg = Rot([P.sb("wstg%d" % i, [128, 1024]) for i in range(2)])
            load_cast(P, Wga, io["w_in"], 0, D, O_GATE, 3072, stg)
            load_cast(P, Wrg, io["w_in"], 0, D, O_GLA_R, 1024, stg)
            load_cast(P, Wrg, io["w_in"], 0, D, O_RET_G, 1024, stg, d0=1024)
            for i in range(3):
                for kb in range(8):
                    s_ = stg()
                    P.dma(s_[:, :], io["w_branch"].v(0, io["w_branch"].ap[i, kb * 128:(kb + 1) * 128, :]))
                    P.cp(Wb[:, i * 8 + kb, :], s_[:, :], eng="dve" if kb % 2 == 0 else "pool")
            P.barrier()
        P.stack = st
        rowb = {}
        for nm, pr in (("lng", PR_LNG), ("lnb", PR_LNB), ("glag", PR_GLAG), ("retg", PR_RETG)):
            rowb[nm] = P.sb("rb_" + nm, [128, D])
            P.dma(rowb[nm][:], io["prow"].v(0, io["prow"].ap[pr, :].partition_broadcast(128)))
        hT1 = P.sb("hT1", [128, 8, 128], BF16)
        bufs = {"xt": Rot([P.sb("xt0", [128, D])]), "xn": Rot([P.sb("xn0", [128, D])]),
                "ss": Rot([P.sb("ss%d" % i, [128, 4]) for i in range(2)]), "ps": Rot([pb[6], pb[7]])}
        o_t = P.sb("o_t", [128, D]); aux1 = P.sb("aux1", [128, D]); aux2 = P.sb("aux2", [128, D])
        y_t = P.sb("y_t", [128, D]); tmp = P.sb("tmp", [128, D]); sig = P.sb("sig", [128, D])
        mrg = P.sb("mrg", [128, D]); stt_ = P.sb("stats", [128, 64])
        yT = P.sb("yT", [128, 8, 128], BF16)
        for ti in range(cfg.NTL):
            rows = slice(ti * 128, (ti + 1) * 128)
            build_hT(L, hT1[:, :, :], ti, L.G1, L.S1, io["xin"], bufs)
            for br in range(3):
                for hf in range(2):
                    pg = pb[hf]
                    c0 = br * 1024 + hf * 512
                    for kb in range(8):
                        P.mm(pg[:, :], hT1[:, kb, :], Wga[:, kb, c0:c0 + 512], start=(kb == 0), stop=(kb == 7))
                    P.act(sig[:, hf * 512:(hf + 1) * 512], pg[:, :], AF.Sigmoid)
                if br == 0:
                    P.dma(o_t[:], io["orw"].v(ti, io["orw"].ap[rows, :]))
                    P.dma(aux1[:], io["bonus"].v(ti, io["bonus"].ap[rows, :]), q="pool")
                    P.dma(aux2[:], io["gtok"].v(ti, io["gtok"].ap[rows, :]))
                    head_norm_tok(L, y_t, o_t, 16, L.epsg, True, tmp, stt_)
                    P.tt(y_t[:], y_t[:], rowb["lng"][:], ALU.mult)
                    P.tt(y_t[:], y_t[:], rowb["lnb"][:], ALU.add, eng="pool")
                    P.tt(y_t[:], y_t[:], aux1[:], ALU.add)
                    P.tt(y_t[:], y_t[:], aux2[:], ALU.mult, eng="pool")
                else:
                    src = io["ogl"] if br == 1 else io["ort"]
                    P.dma(o_t[:], src.v(ti, src.ap[rows, :]))
                    for hf in range(2):
                        pg = pb[2 + hf]
                        c0 = (br - 1) * 1024 + hf * 512
                        for kb in range(8):
                            P.mm(pg[:, :], hT1[:, kb, :], Wrg[:, kb, c0:c0 + 512], start=(kb == 0), stop=(kb == 7))
                        P.act(aux1[:, hf * 512:(hf + 1) * 512], pg[:, :], AF.Silu)
                    head_norm_tok(L, y_t, o_t, 4, L.eps5, br == 2, tmp, stt_)
                    P.tt(y_t[:], y_t[:], rowb["glag" if br == 1 else "retg"][:], ALU.mult)
                    P.tt(y_t[:], y_t[:], aux1[:], ALU.mult, eng="pool")
                for half in range(2):
                    ps = pb[4 + half]
                    for q in range(4):
                        kb = half * 4 + q
                        P.tr(ps[:, q * 128:(q + 1) * 128], y_t[:, kb * 128:(kb + 1) * 128], ident)
                    P.cp(V(yT.t[:, half * 4:(half + 1) * 4, :].rearrange("p a b -> p (a b)"), yT.res), ps[:, :],
                         eng="act" if half == 0 else "dve")
                for hf in range(2):
                    pu = pb[2 + hf]
                    for kb in range(8):
                        P.mm(pu[:, :], yT[:, kb, :], Wb[:, br * 8 + kb, hf * 512:(hf + 1) * 512], start=(kb == 0), stop=(kb == 7))
                    cs_ = slice(hf * 512, (hf + 1) * 512)
                    if br == 0:
                        P.tt(mrg[:, cs_], pu[:, :], sig[:, cs_], ALU.mult)
                    else:
                        P.tt(tmp[:, cs_], pu[:, :], sig[:, cs_], ALU.mult)
                        P.tt(mrg[:, cs_], mrg[:, cs_], tmp[:, cs_], ALU.add, eng="pool")
            P.dma(io["mrg"].v(ti, io["mrg"].ap[rows, :]), mrg[:], q="pool")
        P.barrier()
    with ExitStack() as st:
        P.stack = st
        Wo = P.sb("Wo", [128, 8, 1024], BF16)
        W1 = P.sb("W1", [128, 8, 4096], BF16)
        W2 = P.sb("W2", [128, 32, 1024], BF16)
        with ExitStack() as st2:
            P.stack = st2
            stg = Rot([P.sb("wstg%d" % i, [128, 1024]) for i in range(2)])
            load_cast(P, Wo, io["w_out"], 0, D, 0, 1024, stg)
            load_cast(P, W1, io["mlp_w1"], 0, D, 0, 4096, stg)
            load_cast(P, W2, io["mlp_w2"], 0, 4 * D, 0, 1024, stg)
            P.barrier()
        P.stack = st
        A2 = [P.sb("A2_%d" % j, [128, D]) for j in range(2)]
        A5 = [P.sb("A5_%d" % j, [128, D]) for j in range(2)]
        gb = P.sb("gpost", [128, D])
        for (A, off, pr) in ((A2, 2 * D, PR_NPOM), (A5, 5 * D, PR_NPOL)):
            P.dma(gb[:], io["prow"].v(0, io["prow"].ap[pr, :].partition_broadcast(128)))
            for j in range(2):
                P.dma(A[j][:], io["modrow"].v(0, io["modrow"].ap[j, off:off + D].partition_broadcast(128)))
                P.tt(A[j][:], A[j][:], gb[:], ALU.mult)
        xt = P.sb("xt", [128, D]); mg = P.sb("mg", [128, D]); xm = P.sb("xm", [128, D]); xn = P.sb("xn", [128, D])
        tmp = P.sb("tmp", [128, D])
        mT = P.sb("mT", [128, 8, 128], BF16); h2T = P.sb("h2T", [128, 8, 128], BF16)
        hid = P.sb("hid", [128, 32, 128], BF16); rl = P.sb("rl", [128, 512])
        ss = P.sb("ssF", [128, 8])

        def rms_resid(dst, base, pz, A):
            for hf in range(2):
                P.act(tmp[:, hf * 512:(hf + 1) * 512], pz[hf][:, :], AF.Square, accum=ss[:, hf:hf + 1])
            P.tt(ss[:, 2:3], ss[:, 0:1], ss[:, 1:2], ALU.add)
            P.act(ss[:, 3:4], ss[:, 2:3], AF.Sqrt, scale=1.0 / D, bias=L.eps6[:])
            P.recip(ss[:, 4:5], ss[:, 3:4])
            for hf in range(2):
                cs_ = slice(hf * 512, (hf + 1) * 512)
                P.stt(tmp[:, cs_], pz[hf][:, :], ss[:, 4:5], A[:, cs_], ALU.mult, ALU.mult)
                P.tt(dst[:, cs_], base[:, cs_], tmp[:, cs_], ALU.add, eng="pool")

        for ti in range(cfg.NTL):
            rows = slice(ti * 128, (ti + 1) * 128)
            j = 0 if ti < cfg.NCT else 1
            P.dma(xt[:], io["xin"].v(ti, io["xin"].ap[rows, :]))
            P.dma(mg[:], io["mrg"].v(ti, io["mrg"].ap[rows, :]), q="pool")
            for half in range(2):
                ps = pb[half]
                for q in range(4):
                    kb = half * 4 + q
                    P.tr(ps[:, q * 128:(q + 1) * 128], mg[:, kb * 128:(kb + 1) * 128], ident)
                P.cp(V(mT.t[:, half * 4:(half + 1) * 4, :].rearrange("p a b -> p (a b)"), mT.res), ps[:, :],
                     eng="act" if half == 0 else "dve")
            pz = [pb[2], pb[3]]
            for hf in range(2):
                for kb in range(8):
                    P.mm(pz[hf][:, :], mT[:, kb, :], Wo[:, kb, hf * 512:(hf + 1) * 512], start=(kb == 0), stop=(kb == 7))
            rms_resid(xm, xt, pz, A2[j])
            P.act(tmp[:], xm[:], AF.Square, accum=ss[:, 5:6])
            P.act(ss[:, 6:7], ss[:, 5:6], AF.Sqrt, scale=1.0 / D, bias=L.eps6[:])
            P.recip(ss[:, 7:8], ss[:, 6:7])
            P.ts(xn[:], xm[:], ss[:, 7:8], ALU.mult, eng="pool")
            for half in range(2):
                ps = pb[4 + half]
                for q in range(4):
                    kb = half * 4 + q
                    P.tr(ps[:, q * 128:(q + 1) * 128], xn[:, kb * 128:(kb + 1) * 128], ident)
                for q in range(4):
                    kb = half * 4 + q
                    if q % 2 == 0:
                        P.act(h2T[:, kb, :], ps[:, q * 128:(q + 1) * 128], AF.Identity, scale=L.G2[:, kb, j:j + 1], bias=L.S2[:, kb, j:j + 1])
                    else:
                        P.ts(h2T[:, kb, :], ps[:, q * 128:(q + 1) * 128], L.G2[:, kb, j:j + 1], ALU.mult, L.S2[:, kb, j:j + 1], ALU.add)
            for fg in range(8):
                ph = pb[6 + fg % 2]
                for q in range(4):
                    fb = fg * 4 + q
                    for kb in range(8):
                        P.mm(ph[:, q * 128:(q + 1) * 128], W1[:, kb, fb * 128:(fb + 1) * 128], h2T[:, kb, :], start=(kb == 0), stop=(kb == 7))
                P.ts(rl[:], ph[:, :], 0.0, ALU.max)
                P.tt(V(hid.t[:, fg * 4:(fg + 1) * 4, :].rearrange("p a b -> p (a b)"), hid.res), rl[:], rl[:], ALU.mult, eng="pool")
            pm = [pb[0], pb[1]]
            for hf in range(2):
                for fb in range(32):
                    P.mm(pm[hf][:, :], hid[:, fb, :], W2[:, fb, hf * 512:(hf + 1) * 512], start=(fb == 0), stop=(fb == 31))
            rms_resid(xt, xm, pm, A5[j])
            P.dma(io["xout"].v(ti, io["xout"].ap[rows, :]), xt[:], q="pool")
        P.barrier()
    P.stack = None


def build_fused_program(cfg, nl):
    nc = bass.Bass("TRN2", target_bir_lowering=False)
    P = Prog(nc)
    shared = {}

    def dram(name, shape, kind):
        return DT(nc.dram_tensor(name, list(shape), F32, kind=kind).ap(), name)

    for k, shp in (("cmat", [128, 8 * 128]), ("cosT", [128, cfg.NT]), ("sinT", [128, cfg.NT]),
                   ("cosK", [cfg.NT, 128]), ("sinK", [cfg.NT, 128]), ("xin0", [cfg.NT, D]), ("vzero", [8, 128, cfg.NT])):
        shared[k] = dram(k, shp, "ExternalInput")
    for k, f in SCRATCH.items():
        shared[k] = dram(k, f(cfg), "Internal")
    xbuf = [dram("xbuf%d" % i, [cfg.NT, D], "Internal") for i in range(2)]
    vfirst = dram("vfirst", [8, 128, cfg.NT], "Internal")
    vdump = dram("vdump", [8, 128, cfg.NT], "Internal")
    xfinal = dram("xout", [cfg.NT, D], "ExternalOutput")
    for l in range(nl):
        io = dict(shared)
        for k, shp in LAYER_IN_SHAPES.items():
            io[k] = dram("l%d_%s" % (l, k), shp, "ExternalInput")
        io["xin"] = shared["xin0"] if l == 0 else xbuf[(l - 1) % 2]
        io["xout"] = xfinal if l == nl - 1 else xbuf[l % 2]
        io["vfin"] = shared["vzero"] if l == 0 else vfirst
        io["vout"] = vfirst if l == 0 else vdump
        emit_layer(P, cfg, io)
        P.barrier()
    P.finish()
    return nc, P


FUSED = False
N_CORES = 4


def kernel(**inputs):
    cfg = Cfg()
    inputs = {k: np.asarray(v) for k, v in inputs.items()}
    B = inputs["x"].shape[0]
    NL = inputs["w_in"].shape[0]
    consts = host_consts(cfg)
    xs = [np.ascontiguousarray(np.concatenate([inputs["ctx"][b], inputs["x"][b]], 0).astype(np.float32)) for b in range(B)]
    if FUSED:
        nc, P = build_fused_program(cfg, NL)
        in_maps = []
        for core in range(N_CORES):
            b = core % B
            m = dict(consts)
            m["xin0"] = xs[b]
            m["vzero"] = np.zeros((8, 128, cfg.NT), np.float32)
            for l in range(NL):
                for k, v in host_layer_inputs(inputs, l, b, cfg).items():
                    m["l%d_%s" % (l, k)] = v
            in_maps.append(m)
        res = run_bass_kernel_spmd(nc, in_maps, core_ids=list(range(N_CORES)))
        xs = [res.results[b]["xout"] for b in range(B)]
    else:
        nc, P = build_layer_program(cfg)
        vfs = [np.zeros((8, 128, cfg.NT), np.float32) for _ in range(B)]
        for l in range(NL):
            in_maps = []
            for core in range(N_CORES):
                b = core % B
                m = host_layer_inputs(inputs, l, b, cfg)
                m.update(consts)
                m["xin"] = xs[b]
                m["vfin"] = vfs[b]
                in_maps.append(m)
            res = run_bass_kernel_spmd(nc, in_maps, core_ids=list(range(N_CORES)))
            xs = [np.asarray(res.results[b]["xout"]) for b in range(B)]
            if l == 0:
                vfs = [np.asarray(res.results[b]["vout"]) for b in range(B)]
    return np.stack([np.asarray(xs[b])[cfg.NC:] for b in range(B)], 0).astype(np.float32)
```

```python
import numpy as np
import concourse.bass as bass
import concourse.mybir as mybir
from concourse.bass_utils import run_bass_kernel_spmd
from contextlib import ExitStack

F32 = mybir.dt.float32
BF16 = mybir.dt.bfloat16
AF = mybir.ActivationFunctionType
ALU = mybir.AluOpType
AX = mybir.AxisListType

D = 1024
KB = 8
GW = 64
ALPHA = float(np.exp(-0.5))


class Res:
    __slots__ = ("name", "w", "r")

    def __init__(self, name):
        self.name = name
        self.w = None
        self.r = {}


class V:
    __slots__ = ("ap", "res")

    def __init__(self, ap, res):
        self.ap = ap
        self.res = res


class T:
    def __init__(self, t, name):
        self.t = t
        self.res = Res(name)

    def __getitem__(self, k):
        return V(self.t[k], self.res)

    def v(self, ap):
        return V(ap, self.res)


class DT:
    def __init__(self, ap, name):
        self.ap = ap
        self.name = name
        self.rs = {}

    def v(self, key, ap):
        r = self.rs.get(key)
        if r is None:
            r = self.rs[key] = Res("%s/%s" % (self.name, key))
        return V(ap, r)


class Prog:
    ENG = ("pe", "dve", "act", "pool", "sp")
    LIMIT = 16000

    def __init__(self, nc, n_dma_slots=6):
        self.nc = nc
        self.eng = {"pe": nc.tensor, "dve": nc.vector, "act": nc.scalar, "pool": nc.gpsimd, "sp": nc.sync}
        self.sem = {}
        self.cnt = {}
        self.keyeng = {}
        self.cur = {}
        self.nkeys = 0
        for e in self.ENG:
            self.cur[e] = self._newkey(e)
        self.dslots = {}
        self.dnext = {}
        for q in ("sp", "pool"):
            self.dslots[q] = [self._newkey("dma") for i in range(n_dma_slots)]
            self.dnext[q] = 0
        self.waited = {e: {} for e in self.ENG}
        self.ninst = 0
        self.stack = None

    def _newkey(self, e):
        self.nkeys += 1
        k = "%s#%d" % (e, self.nkeys)
        self.sem[k] = self.nc.alloc_semaphore(name="s_%s_%d" % (e, self.nkeys))
        self.cnt[k] = 0
        self.keyeng[k] = e
        return k

    def sb(self, name, shape, dt=F32):
        self.uid = getattr(self, "uid", 0) + 1
        name = "sb%d_%s" % (self.uid, name)
        t = self.stack.enter_context(self.nc.sbuf_tensor(name, list(shape), dt))
        return T(t, name)

    def ps(self, name, shape, dt=F32):
        self.uid = getattr(self, "uid", 0) + 1
        name = "ps%d_%s" % (self.uid, name)
        t = self.stack.enter_context(self.nc.psum_tensor(name, list(shape), dt))
        return T(t, name)

    def _wait(self, e, key, val):
        if val <= 0:
            return
        if e == "pe" and self.keyeng[key] == "pe":
            return
        w = self.waited[e]
        if w.get(key, 0) >= val:
            return
        self.eng[e].wait_ge(self.sem[key], val)
        w[key] = val

    def _deps(self, e, reads, writes):
        for r in reads:
            if r.w is not None:
                self._wait(e, r.w[0], r.w[1])
        for r in writes:
            if r.w is not None:
                self._wait(e, r.w[0], r.w[1])
            for k, v in r.r.items():
                self._wait(e, k, v)

    def _mark(self, key, val, reads, writes):
        for r in reads:
            if r.r.get(key, 0) < val:
                r.r[key] = val
        for r in writes:
            r.w = (key, val)
            r.r = {}

    def op(self, e, fn, reads=(), writes=()):
        reads = [x.res for x in reads if x is not None and not isinstance(x, (int, float))]
        writes = [x.res for x in writes]
        self._deps(e, reads, writes)
        ins = fn(self.eng[e])
        k = self.cur[e]
        if self.cnt[k] >= self.LIMIT:
            k = self.cur[e] = self._newkey(e)
        self.cnt[k] += 1
        ins.then_inc(self.sem[k], 1)
        self._mark(k, self.cnt[k], reads, writes)
        self.ninst += 1
        return ins

    def dma(self, out, in_, q="sp", **kw):
        slots = self.dslots[q]
        si = self.dnext[q] % len(slots)
        k = slots[si]
        self.dnext[q] += 1
        self._wait(q, k, self.cnt[k])
        if self.cnt[k] >= self.LIMIT:
            k = slots[si] = self._newkey("dma")
        reads = [in_.res]
        writes = [out.res]
        self._deps(q, reads, writes)
        ins = self.eng[q].dma_start(out=out.ap, in_=in_.ap, **kw)
        self.cnt[k] += 16
        ins.then_inc(self.sem[k], 16)
        self._mark(k, self.cnt[k], reads, writes)
        self.ninst += 1

    def barrier(self):
        for e in self.ENG:
            for k in self.sem:
                self._wait(e, k, self.cnt[k])

    def mm(self, out, lhsT, rhs, start=True, stop=True):
        self.op("pe", lambda e: e.matmul(out.ap, lhsT=lhsT.ap, rhs=rhs.ap, start=start, stop=stop),
                reads=[lhsT, rhs], writes=[out])

    def tr(self, out, in_, ident):
        self.op("pe", lambda e: e.transpose(out=out.ap, in_=in_.ap, identity=ident.ap),
                reads=[in_, ident], writes=[out])

    def act(self, out, in_, func, scale=1.0, bias=None, accum=None, eng="act"):
        kw = {}
        rd = [in_]
        sc = scale
        if isinstance(scale, V):
            sc = scale.ap
            rd.append(scale)
        if bias is not None:
            if isinstance(bias, V):
                kw["bias"] = bias.ap
                rd.append(bias)
            else:
                kw["bias"] = bias
        wr = [out]
        if accum is not None:
            kw["accum_out"] = accum.ap
            wr.append(accum)
        self.op(eng, lambda e: e.activation(out=out.ap, in_=in_.ap, func=func, scale=sc, **kw), reads=rd, writes=wr)

    def tt(self, out, a, b, op, eng="dve"):
        self.op(eng, lambda e: e.tensor_tensor(out=out.ap, in0=a.ap, in1=b.ap, op=op), reads=[a, b], writes=[out])

    def ts(self, out, a, s1, op0, s2=None, op1=None, eng="dve"):
        rd = [a]
        x1 = s1
        if isinstance(s1, V):
            x1 = s1.ap
            rd.append(s1)
        x2 = s2
        if isinstance(s2, V):
            x2 = s2.ap
            rd.append(s2)
        if op1 is None:
            self.op(eng, lambda e: e.tensor_scalar(out=out.ap, in0=a.ap, scalar1=x1, scalar2=None, op0=op0),
                    reads=rd, writes=[out])
        else:
            self.op(eng, lambda e: e.tensor_scalar(out=out.ap, in0=a.ap, scalar1=x1, scalar2=x2, op0=op0, op1=op1),
                    reads=rd, writes=[out])

    def stt(self, out, in0, scalar, in1, op0, op1):
        rd = [in0, in1]
        x = scalar
        if isinstance(scalar, V):
            x = scalar.ap
            rd.append(scalar)
        self.op("dve", lambda e: e.scalar_tensor_tensor(out=out.ap, in0=in0.ap, scalar=x, in1=in1.ap, op0=op0, op1=op1),
                reads=rd, writes=[out])

    def cp(self, out, in_, eng="dve"):
        if eng == "act":
            self.op(eng, lambda e: e.activation(out=out.ap, in_=in_.ap, func=AF.Copy), reads=[in_], writes=[out])
        else:
            self.op(eng, lambda e: e.tensor_copy(out=out.ap, in_=in_.ap), reads=[in_], writes=[out])

    def memset(self, out, val, eng="pool"):
        self.op(eng, lambda e: e.memset(out.ap, val), writes=[out])

    def recip(self, out, in_):
        self.op("dve", lambda e: e.reciprocal(out=out.ap, in_=in_.ap), reads=[in_], writes=[out])

    def scan(self, out, d0, d1, init, op0, op1):
        self.op("dve", lambda e: e.tensor_tensor_scan(out=out.ap, data0=d0.ap, data1=d1.ap, initial=init, op0=op0, op1=op1),
                reads=[d0, d1], writes=[out])

    def reduce(self, out, in_, op=ALU.add, axis=AX.X):
        self.op("dve", lambda e: e.tensor_reduce(out=out.ap, in_=in_.ap, axis=axis, op=op), reads=[in_], writes=[out])

    def finish(self):
        for k in self.sem:
            self._wait("sp", k, self.cnt[k])


PC_MU = 0
PC_W0F = 27
PC_W0B = 35
PC_A0 = 43
PC_V0 = 51
PC_KK = 59
PC_KA = 67
PC_RK = 75
PC_NPM = 83
PC_NPL = 91
PC_SLOT = 99
PC_RETD = 103
PC_FLAG = 111
PC_IDX = 112
PC_RIDX = 113
PC_IDX1 = 114
PC_RIDX1 = 115
NPC = 116

PR_LNG, PR_LNB, PR_GLAG, PR_RETG, PR_NPOM, PR_NPOL = range(6)
NPR = 6

CM_SF, CM_IF, CM_SB, CM_IB, CM_ID, CM_DF, CM_DB, CM_BO = range(8)


class Cfg:
    def __init__(self, S=4096, NC=256):
        self.S = S
        self.NC = NC
        self.NT = S + NC
        self.NTL = self.NT // 128
        self.NCT = NC // 128
        self.ROWS = S // GW
        self.NG = S // 256


def host_consts(cfg):
    idx = np.arange(128)
    s = idx[:, None]
    c = idx[None, :]
    cm = np.zeros((128, 8, 128), np.float32)
    cm[:, CM_SF] = (s < c)
    cm[:, CM_IF] = (s <= c)
    cm[:, CM_SB] = (s > c)
    cm[:, CM_IB] = (s >= c)
    cm[:, CM_ID] = (s == c)
    cm[:, CM_DF] = np.maximum(c - s, 0)
    cm[:, CM_DB] = np.maximum(s - c, 0)
    cm[:, CM_BO] = ((s // 64) == (c // 64))
    half = 128
    inv_freq = (10000.0 ** (-np.arange(half, dtype=np.float32) / half)).astype(np.float32)
    pos = np.arange(cfg.NT, dtype=np.float32)
    ang = (pos[:, None] * inv_freq[None, :]).astype(np.float32)
    cosT = np.ascontiguousarray(np.cos(ang).astype(np.float32).T)
    sinT = np.ascontiguousarray(np.sin(ang).astype(np.float32).T)
    return {"cmat": cm.reshape(128, 8 * 128), "cosT": cosT, "sinT": sinT,
            "cosK": np.ascontiguousarray(cosT.T), "sinK": np.ascontiguousarray(sinT.T)}


def col(vec):
    v = np.asarray(vec, np.float32).reshape(-1, 128)
    return np.ascontiguousarray(v.T)


def host_layer_inputs(inp, l, b, cfg):
    f = lambda a: np.ascontiguousarray(np.asarray(a, np.float32))
    w_in = f(inp["w_in"][l])
    o_rw = 0
    lo = o_rw + 3072
    w_lora = np.zeros((D, 384), np.float32)
    w_lora[:, 0:192] = w_in[:, lo:lo + 192]
    w_lora[:, 256:384] = w_in[:, lo + 192:lo + 320]
    o_gla = 3392
    ga = o_gla + 512 + 512 + 1024 + 1024
    w_ad = np.zeros((D, 64), np.float32)
    w_ad[:, 0:16] = w_in[:, ga:ga + 16]
    w_ad[:, 32:48] = w_in[:, ga + 16:ga + 32]
    mu = f(inp["rwkv_mu"][l])
    mu_l = np.zeros(384, np.float32)
    mu_l[0:192] = mu[3072:3264]
    mu_l[256:384] = mu[3264:3392]
    pc = np.zeros((128, NPC), np.float32)
    pc[:, PC_MU:PC_MU + 24] = col(mu[0:3072])
    pc[:, PC_MU + 24:PC_MU + 27] = col(mu_l)
    pc[:, PC_W0F:PC_W0F + 8] = col(inp["rwkv_w0"][l][0])
    pc[:, PC_W0B:PC_W0B + 8] = col(inp["rwkv_w0"][l][1])
    pc[:, PC_A0:PC_A0 + 8] = col(inp["rwkv_a0"][l])
    if l > 0:
        pc[:, PC_V0:PC_V0 + 8] = col(inp["rwkv_v0"][l - 1])
    pc[:, PC_KK:PC_KK + 8] = col(inp["rwkv_k_k"][l])
    pc[:, PC_KA:PC_KA + 8] = col(inp["rwkv_k_a"][l])
    pc[:, PC_RK:PC_RK + 8] = col(np.asarray(inp["rwkv_r_k"][l]).reshape(-1))
    pc[:, PC_NPM:PC_NPM + 8] = col(inp["norm_pre_mix"][l])
    pc[:, PC_NPL:PC_NPL + 8] = col(inp["norm_pre_mlp"][l])
    for j in range(4):
        pc[:, PC_SLOT + j] = (np.arange(128) % 4 == j)
    rd = np.asarray(inp["ret_decay"][l], np.float32).reshape(-1)
    pc[:, PC_RETD:PC_RETD + 8] = rd[None, :]
    pc[:, PC_FLAG] = 1.0 if l > 0 else 0.0
    pc[:, PC_IDX] = np.arange(128)
    pc[:, PC_RIDX] = 127 - np.arange(128)
    pc[:, PC_IDX1] = np.arange(128) + 1
    pc[:, PC_RIDX1] = 128 - np.arange(128)
    pr = np.stack([f(inp["rwkv_ln_g"][l]), f(inp["rwkv_ln_b"][l]), f(inp["gla_norm_g"][l]),
                   f(inp["ret_norm_g"][l]), f(inp["norm_post_mix"][l]), f(inp["norm_post_mlp"][l])], 0)
    cT = np.zeros((128, 16), np.float32)
    cc = col(inp["c_ctx"])
    cl = col(inp["c"][b])
    cT[:, 0::2] = cc
    cT[:, 1::2] = cl
    w2s = np.concatenate([f(inp["rwkv_w2"][l][0]), f(inp["rwkv_w2"][l][1])], 0)
    ga2 = np.zeros((64, 512), np.float32)
    ga2[0:16] = inp["gla_a2"][l][0]
    ga2[32:48] = inp["gla_a2"][l][1]
    gab = np.concatenate([f(inp["gla_a_b"][l][0]), f(inp["gla_a_b"][l][1])])[None, :]
    if l > 0:
        v1 = f(inp["rwkv_v1"][l - 1])
        v2 = f(inp["rwkv_v2"][l - 1])
    else:
        v1 = np.zeros((D, 32), np.float32)
        v2 = np.zeros((32, D), np.float32)
    return {
        "cT": cT, "ada_w": f(inp["ada_w"][l]), "ada_b": f(inp["ada_b"][l])[None, :],
        "w_in": w_in, "w_lora": w_lora, "w_ad": w_ad, "pcol": pc, "prow": np.ascontiguousarray(pr),
        "w2s": np.ascontiguousarray(w2s), "a2": f(inp["rwkv_a2"][l]), "g2": f(inp["rwkv_g2"][l]),
        "v1": v1, "v2": v2, "gla_a2": ga2, "gla_ab": np.ascontiguousarray(gab),
        "w_branch": f(inp["w_branch"][l]), "w_out": f(inp["w_out"][l]),
        "mlp_w1": f(inp["mlp_w1"][l]), "mlp_w2": f(inp["mlp_w2"][l]),
    }


LAYER_IN_SHAPES = {
    "cT": [128, 16], "ada_w": [D, 6 * D], "ada_b": [1, 6 * D], "w_in": [D, 13664], "w_lora": [D, 384],
    "w_ad": [D, 64], "pcol": [128, NPC], "prow": [NPR, D], "w2s": [128, D], "a2": [64, D], "g2": [128, D],
    "v1": [D, 32], "v2": [32, D], "gla_a2": [64, 512], "gla_ab": [1, 1024], "w_branch": [3, D, D],
    "w_out": [D, D], "mlp_w1": [D, 4 * D], "mlp_w2": [4 * D, D],
}


class Rot:
    def __init__(self, items):
        self.items = list(items)
        self.i = 0

    def __call__(self):
        x = self.items[self.i % len(self.items)]
        self.i += 1
        return x


def load_cast(P, dst, src_dt, key, rows, c0, ncols, stage_rot, engs=("dve", "pool"), d0=0, r0=0):
    nkb = rows // 128
    i = 0
    for kb in range(nkb):
        for cc in range(0, ncols, 1024):
            n = min(1024, ncols - cc)
            stg = stage_rot()
            P.dma(stg[:, 0:n], src_dt.v(key, src_dt.ap[r0 + kb * 128:r0 + (kb + 1) * 128, c0 + cc:c0 + cc + n]),
                  q="sp" if i % 2 == 0 else "pool")
            P.cp(dst[:, kb, d0 + cc:d0 + cc + n], stg[:, 0:n], eng=engs[i % len(engs)])
            i += 1


class LayerCtx:
    pass


def emit_layer(P, cfg, io, dbg=None):
    NT, NTL, NCT = cfg.NT, cfg.NTL, cfg.NCT
    L = LayerCtx()
    L.P, L.cfg, L.io, L.dbg = P, cfg, io, dbg
    with ExitStack() as st0:
        P.stack = st0
        L.pb = [P.ps("pb%d" % i, [128, 512]) for i in range(8)]
        pcol = L.pcol = P.sb("pcol", [128, NPC])
        P.dma(pcol[:], io["pcol"].v(0, io["pcol"].ap[:, :]))
        cmat = L.cmat = P.sb("cmat", [128, 8, 128])
        P.dma(cmat[:], io["cmat"].v(0, io["cmat"].ap.rearrange("p (a b) -> p a b", a=8)))
        L.ident = cmat[:, CM_ID, :]
        ones = L.ones = P.sb("ones", [128, 128])
        P.memset(ones[:], 1.0)
        L.eps6 = P.sb("eps6", [128, 1]); P.memset(L.eps6[:], 1e-6)
        L.eps5 = P.sb("eps5", [128, 1]); P.memset(L.eps5[:], 1e-5)
        L.epsg = P.sb("epsg", [128, 1]); P.memset(L.epsg[:], 64e-5)
        dc = L.dc = P.sb("dcol", [128, 8, 27])
        mu = pcol[:, PC_MU:PC_MU + 27]
        P.ts(dc[:, 0, :], mu, -1.0, ALU.mult, 1.0, ALU.add)
        for j in range(4):
            P.ts(dc[:, 1 + j, :], mu, pcol[:, PC_SLOT + j:PC_SLOT + j + 1], ALU.mult)
        P.tt(dc[:, 5, :], dc[:, 1, :], dc[:, 3, :], ALU.add)
        P.tt(dc[:, 6, :], dc[:, 2, :], dc[:, 4, :], ALU.add)
        L.omka = P.sb("omka", [128, 8])
        P.ts(L.omka[:], pcol[:, PC_KA:PC_KA + 8], -1.0, ALU.mult, 1.0, ALU.add)
        L.modcol = P.sb("modcol", [128, 48, 2])
        L.G1 = P.sb("G1", [128, 8, 2]); L.S1 = P.sb("S1", [128, 8, 2])
        L.G2 = P.sb("G2", [128, 8, 2]); L.S2 = P.sb("S2", [128, 8, 2])
        emit_mod(L)
        P.barrier()
        stages = {"rwkv": ["rw0", "rw1"], "only_gla": ["gl0", "gl1"], "only_ret": ["rt0", "rt1"], "only_post": ["post"],
                  "only_gla0": ["gl0"], "only_ret0": ["rt0"]}.get(dbg, ["rw0", "rw1", "gl0", "gl1", "rt0", "rt1", "post"])
        for sname in stages:
            if sname[:2] == "rw":
                emit_rwkv(L, int(sname[2]))
            elif sname[:2] == "gl":
                emit_gla(L, int(sname[2]))
            elif sname[:2] == "rt":
                emit_ret(L, int(sname[2]))
            else:
                emit_post(L)
            P.barrier()
    P.stack = None


def emit_mod(L):
    P, io = L.P, L.io
    with ExitStack() as st:
        P.stack = st
        pb = L.pb
        cT = P.sb("cT", [128, 16]); P.dma(cT[:], io["cT"].v(0, io["cT"].ap[:, :]))
        sc = P.sb("sc", [128, 16]); P.act(sc[:], cT[:], AF.Silu)
        adab = P.sb("adab", [2, 6 * D])
        P.dma(adab[:], io["ada_b"].v(0, io["ada_b"].ap[0, :].partition_broadcast(2)))
        modrow = P.sb("modrow", [2, 6 * D])
        stg = Rot([P.sb("adstg%d" % i, [128, 8, 512]) for i in range(2)])
        for cg in range(12):
            s = stg()
            for kb in range(8):
                P.dma(s[:, kb, :], io["ada_w"].v(0, io["ada_w"].ap[kb * 128:(kb + 1) * 128, cg * 512:(cg + 1) * 512]),
                      q="sp" if kb % 2 == 0 else "pool")
            ps = pb[cg % 2]
            for kb in range(8):
                P.mm(ps[0:2, :], sc[:, kb * 2:kb * 2 + 2], s[:, kb, :], start=(kb == 0), stop=(kb == 7))
            P.tt(modrow[:, cg * 512:(cg + 1) * 512], ps[0:2, :], adab[:, cg * 512:(cg + 1) * 512], ALU.add)
        pt = pb[2]
        for blk in range(48):
            P.tr(pt[:, blk * 2:blk * 2 + 2], modrow[0:2, blk * 128:(blk + 1) * 128], L.cmat[0:2, CM_ID, 0:2])
        P.cp(L.modcol.v(L.modcol.t[:].rearrange("p a b -> p (a b)")), pt[:, 0:96])
        mc = L.modcol
        pcol = L.pcol
        for j in range(2):
            P.stt(L.G1[:, :, j], mc[:, 8:16, j], 1.0, pcol[:, PC_NPM:PC_NPM + 8], ALU.add, ALU.mult)
            P.cp(L.S1[:, :, j], mc[:, 0:8, j])
            P.stt(L.G2[:, :, j], mc[:, 32:40, j], 1.0, pcol[:, PC_NPL:PC_NPL + 8], ALU.add, ALU.mult)
            P.cp(L.S2[:, :, j], mc[:, 24:32, j])
        P.dma(io["modrow"].v(0, io["modrow"].ap[:, :]), modrow[:])
        P.barrier()
    P.stack = None


def build_hT(L, dst, ti, G, S, src_dt, bufs):
    P = L.P
    j = 0 if ti < L.cfg.NCT else 1
    xt, xn, ss = bufs["xt"](), bufs["xn"](), bufs["ss"]()
    junk = xn
    P.dma(xt[:], src_dt.v(ti, src_dt.ap[ti * 128:(ti + 1) * 128, :]))
    P.act(junk[:], xt[:], AF.Square, accum=ss[:, 0:1])
    P.act(ss[:, 1:2], ss[:, 0:1], AF.Sqrt, scale=1.0 / D, bias=L.eps6[:])
    P.recip(ss[:, 2:3], ss[:, 1:2])
    P.ts(xn[:], xt[:], ss[:, 2:3], ALU.mult, eng="pool")
    for half in range(2):
        ps = bufs["ps"]()
        for q in range(4):
            kb = half * 4 + q
            P.tr(ps[:, q * 128:(q + 1) * 128], xn[:, kb * 128:(kb + 1) * 128], L.ident)
        for q in range(4):
            kb = half * 4 + q
            o = V(dst.ap[:, kb, :], dst.res)
            if q % 2 == 0:
                P.act(o, ps[:, q * 128:(q + 1) * 128], AF.Identity, scale=G[:, kb, j:j + 1], bias=S[:, kb, j:j + 1])
            else:
                P.ts(o, ps[:, q * 128:(q + 1) * 128], G[:, kb, j:j + 1], ALU.mult, S[:, kb, j:j + 1], ALU.add)
    return xt


def emit_rwkv(L, d):
    P, cfg, io, pb, pcol, cmat = L.P, L.cfg, L.io, L.pb, L.pcol, L.cmat
    NCT, NTL, NG = cfg.NCT, cfg.NTL, cfg.NG
    ident = L.ident
    with ExitStack() as st:
        P.stack = st
        Wr = P.sb("Wr", [128, 8, 3456], BF16)
        with ExitStack() as st2:
            P.stack = st2
            stg = Rot([P.sb("wstg%d" % i, [128, 1024]) for i in range(2)])
            load_cast(P, Wr, io["w_in"], 0, D, 0, 3072, stg)
            load_cast(P, Wr, io["w_lora"], 0, D, 0, 384, stg, d0=3072)
            P.barrier()
        P.stack = st
        w2s = P.sb("w2s", [128, D]); P.memset(w2s[:], 0.0)
        P.dma(w2s[d * 64:(d + 1) * 64, :], io["w2s"].v(0, io["w2s"].ap[d * 64:(d + 1) * 64, :]))
        a2 = P.sb("a2", [64, D]); P.dma(a2[:], io["a2"].v(0, io["a2"].ap[:, :]))
        g2 = P.sb("g2", [128, D]); P.dma(g2[:], io["g2"].v(0, io["g2"].ap[:, :]))
        v1 = P.sb("v1", [128, 8, 32]); P.dma(v1[:], io["v1"].v(0, io["v1"].ap.rearrange("(kb p) c -> p kb c", p=128)))
        v2 = P.sb("v2", [32, D]); P.dma(v2[:], io["v2"].v(0, io["v2"].ap[:, :]))
        cS, cI = (CM_SF, CM_IF) if d == 0 else (CM_SB, CM_IB)
        cX = CM_SB if d == 0 else CM_SF
        maskMA = P.sb("maskMA", [128, 2, 2, 128]); maskNB = P.sb("maskNB", [128, 2, 2, 128])
        maskX = P.sb("maskX", [128, 2, 128])
        for h in range(2):
            P.cp(maskMA[:, h, 0, :], cmat[:, cS, :]); P.cp(maskMA[:, h, 1, :], cmat[:, cI, :])
            P.ts(maskNB[:, h, 0, :], cmat[:, cS, :], -1.0, ALU.mult); P.cp(maskNB[:, h, 1, :], cmat[:, cI, :])
            P.ts(maskX[:, h, :], cmat[:, cX, :], -1.0, ALU.mult)
        Hbd = P.sb("Hbd", [128, 8, 128]); P.memset(Hbd[:], 0.0)
        GTbd = P.sb("GTbd", [128, 2, 128]); P.memset(GTbd[:], 0.0)
        dHbd = P.sb("dHbd", [128, 2, 128]); P.memset(dHbd[:], 0.0)
        hT4 = P.sb("hT4", [128, 8, 512], BF16)
        P.memset(hT4[:], 0.0)
        xt0 = P.sb("xt0", [128, D])
        bufs = {"xt": Rot([xt0]), "xn": Rot([P.sb("xn0", [128, D])]),
                "ss": Rot([P.sb("ss%d" % i, [128, 4]) for i in range(2)]),
                "ps": Rot([pb[6], pb[7]])}
        pp = P.sb("pp", [128, 27, 256])
        tw = P.sb("tw", [128, 256]); sgd = P.sb("sgd", [128, 256]); vv1 = P.sb("vv1", [32, 256])
        EW = []
        for i in range(2):
            e = {}
            for nm in ("a", "t1", "vp", "kap", "kpr", "b", "sg", "cs", "pre", "suf", "Ea", "Eb", "t2"):
                e[nm] = P.sb("ew%d_%s" % (i, nm), [128, 256])
            e["PC"] = P.sb("ew%d_PC" % i, [128, 2])
            e["khat"], e["bhat"], e["kbar"], e["bbar"] = e["sg"], e["cs"], e["a"], e["t2"]
            e["KR"] = P.sb("ew%d_KR" % i, [128, 2, 256])
            EW.append(e)
        MA = [P.sb("MA%d" % i, [128, 2, 2, 128]) for i in range(2)]
        NB = [P.sb("NB%d" % i, [128, 2, 2, 128]) for i in range(2)]
        Xa = [P.sb("Xa%d" % i, [128, 4, 128]) for i in range(2)]
        XTa = [P.sb("XTa%d" % i, [128, 4, 128]) for i in range(2)]
        Y = P.sb("Yut", [128, 4, 128])
        NU = P.sb("NUut", [128, 2, 4, 64])
        TM = [P.sb("TM%d" % i, [128, 3, 128]) for i in range(2)]
        QE = P.sb("QE", [128, 2, 128])
        o_sb = [P.sb("o_sb%d" % i, [128, D]) for i in range(2)]
        o_prev = xt0
        bon = P.sb("bon", [128, 8, 256]) if d == 0 else None

        groups = [("ctx", 0, [0, 1])] + [("lat", g, [NCT + 2 * g, NCT + 2 * g + 1]) for g in range(NG)]
        if d == 1:
            groups = [groups[0]] + groups[1:][::-1]
        G1, S1 = L.G1, L.S1
        om = lambda blk: L.dc[:, 0, blk:blk + 1]
        mus = lambda j, blk: L.dc[:, 1 + j, blk:blk + 1]
        for (kind, g, tiles) in groups:
            if kind == "ctx":
                for i, ti in enumerate(tiles):
                    build_hT(L, hT4[:, :, (1 + i) * 128:(2 + i) * 128], ti, G1, S1, io["xin"], bufs)
                w0, wn = 128, 256
            else:
                t0 = tiles[0]
                for i in range(4):
                    ti = t0 - 1 + i
                    if (g == 0 and i == 0) or (g == NG - 1 and i == 3):
                        continue
                    build_hT(L, hT4[:, :, i * 128:(i + 1) * 128], ti, G1, S1, io["xin"], bufs)
                w0, wn = 64, 384
            for blk in range(27):
                ps = pb[blk % 2]
                for kb in range(8):
                    P.mm(ps[:, 0:wn], Wr[:, kb, blk * 128:(blk + 1) * 128], hT4[:, kb, w0:w0 + wn],
                         start=(kb == 0), stop=(kb == 7))
                if kind == "ctx":
                    ov = pp[:, blk, :]
                    P.act(ov, ps[:, 0:256], AF.Identity, scale=om(blk))
                    P.stt(pp[:, blk, 1:256], ps[:, 0:255], L.dc[:, 5, blk:blk + 1], pp[:, blk, 1:256], ALU.mult, ALU.add)
                    P.stt(pp[:, blk, 0:255], ps[:, 1:256], L.dc[:, 6, blk:blk + 1], pp[:, blk, 0:255], ALU.mult, ALU.add)
                else:
                    pv = ps.t[:, 0:384].rearrange("p (r c) -> p r c", c=64)
                    ovt = pp.t[:, blk, :].rearrange("p (r c) -> p r c", c=64)
                    pvv = lambda ap: V(ap, ps.res)
                    ovv = lambda ap: V(ap, pp.res)
                    P.act(ovv(ovt), pvv(pv[:, 1:5, :]), AF.Identity, scale=om(blk))
                    P.stt(ovv(ovt[:, :, 1:64]), pvv(pv[:, 1:5, 0:63]), mus(0, blk), ovv(ovt[:, :, 1:64]), ALU.mult, ALU.add)
                    P.stt(ovv(ovt[:, :, 0:63]), pvv(pv[:, 1:5, 1:64]), mus(1, blk), ovv(ovt[:, :, 0:63]), ALU.mult, ALU.add)
                    ql = 1 if g == 0 else 0
                    P.stt(ovv(ovt[:, ql:4, :]), pvv(pv[:, ql:4, :]), mus(2, blk), ovv(ovt[:, ql:4, :]), ALU.mult, ALU.add)
                    qh = 3 if g == NG - 1 else 4
                    P.stt(ovv(ovt[:, 0:qh, :]), pvv(pv[:, 2:2 + qh, :]), mus(3, blk), ovv(ovt[:, 0:qh, :]), ALU.mult, ALU.add)
            tok0 = tiles[0] * 128
            P.act(tw[:], pp[:, 24, :], AF.Tanh)
            if d == 0:
                P.act(sgd[:], pp[:, 26, :], AF.Sigmoid)
            pv1 = pb[2]
            for cb in range(8):
                P.mm(pv1[0:32, 0:256], v1[:, cb, :], pp[:, 16 + cb, :], start=(cb == 0), stop=(cb == 7))
            P.cp(vv1[:], pv1[0:32, 0:256])
            chunks = [0, 1] if d == 0 else [1, 0]
            for quad in range(4):
                for i in range(2):
                    cb = 2 * quad + i
                    E = EW[i]
                    r_, k_, v_ = pp[:, cb, :], pp[:, 8 + cb, :], pp[:, 16 + cb, :]
                    pc = lambda base: pcol[:, base + cb:base + cb + 1]
                    pa = pb[2 + i]
                    P.mm(pa[:, 0:256], a2[:, cb * 128:(cb + 1) * 128], pp[0:64, 25, :])
                    P.act(E["a"][:], pa[:, 0:256], AF.Sigmoid, bias=pc(PC_A0))
                    P.mm(pa[:, 256:512], v2[:, cb * 128:(cb + 1) * 128], vv1[:])
                    P.act(E["t1"][:], pa[:, 256:512], AF.Sigmoid, bias=pc(PC_V0))
                    P.dma(E["t2"][:], io["vfin"].v((cb, tiles[0]), io["vfin"].ap[cb, :, tok0:tok0 + 256]))
                    P.tt(E["t2"][:], E["t2"][:], v_, ALU.subtract)
                    P.stt(E["t2"][:], E["t2"][:], pcol[:, PC_FLAG:PC_FLAG + 1], E["t1"][:], ALU.mult, ALU.mult)
                    P.tt(E["vp"][:], E["t2"][:], v_, ALU.add)
                    if d == 0:
                        P.dma(io["vout"].v((cb, tiles[0]), io["vout"].ap[cb, :, tok0:tok0 + 256]), E["vp"][:], q="pool")
                    P.ts(E["kap"][:], k_, pc(PC_KK), ALU.mult, eng="pool")
                    P.tt(E["t1"][:], E["kap"][:], E["kap"][:], ALU.mult, eng="pool")
                    pq = pb[4 + i]
                    P.mm(pq[:, 0:256], cmat[:, CM_BO, :], E["t1"][:])
                    P.ts(E["t1"][:], pq[:, 0:256], 1e-12, ALU.max)
                    P.act(E["t1"][:], E["t1"][:], AF.Sqrt)
                    P.recip(E["t1"][:], E["t1"][:])
                    P.tt(E["kap"][:], E["kap"][:], E["t1"][:], ALU.mult)
                    P.ts(E["t1"][:], E["a"][:], pc(PC_KA), ALU.mult, L.omka[:, cb:cb + 1], ALU.add)
                    P.tt(E["kpr"][:], k_, E["t1"][:], ALU.mult)
                    P.tt(E["b"][:], E["kap"][:], E["a"][:], ALU.mult, eng="pool")
                    if d == 0:
                        P.stt(E["t1"][:], r_, pc(PC_RK), E["kpr"][:], ALU.mult, ALU.mult)
                        P.mm(pq[:, 256:512], cmat[:, CM_BO, :], E["t1"][:])
                        P.tt(bon[:, cb, :], pq[:, 256:512], E["vp"][:], ALU.mult)
                    P.mm(pq[:, 0:256], w2s[:, cb * 128:(cb + 1) * 128], tw[:])
                    P.act(E["sg"][:], pq[:, 0:256], AF.Sigmoid, bias=pc(PC_W0F if d == 0 else PC_W0B))
                    for c in range(2):
                        cs_ = slice(c * 128, (c + 1) * 128)
                        P.scan(E["cs"][:, cs_], L.ones[:, 0:128], E["sg"][:, cs_], 0.0, ALU.mult, ALU.add)
                    P.tt(E["pre"][:], E["cs"][:], E["sg"][:], ALU.subtract, eng="pool")
                    for c in range(2):
                        cs_ = slice(c * 128, (c + 1) * 128)
                        P.ts(E["suf"][:, cs_], E["cs"][:, cs_], -1.0, ALU.mult,
                             E["cs"][:, c * 128 + 127:c * 128 + 128], ALU.add)
                    if d == 0:
                        incl, excl, lo = E["cs"], E["pre"], E["suf"]
                    else:
                        P.tt(E["t1"][:], E["suf"][:], E["sg"][:], ALU.add, eng="pool")
                        incl, excl, lo = E["t1"], E["suf"], E["pre"]
                    P.act(E["Ea"][:], incl[:], AF.Exp, scale=-ALPHA)
                    for c in range(2):
                        pcc = (c * 128 + 127) if d == 0 else (c * 128)
                        P.cp(E["PC"][:, c:c + 1], E["Ea"][:, pcc:pcc + 1], eng="pool")
                    P.tt(E["KR"][:, 1, :], r_, E["Ea"][:], ALU.mult, eng="pool")
                    P.act(E["Eb"][:], incl[:], AF.Exp, scale=ALPHA)
                    P.tt(E["khat"][:], E["kpr"][:], E["Eb"][:], ALU.mult)
                    P.tt(E["bhat"][:], E["b"][:], E["Eb"][:], ALU.mult, eng="pool")
                    P.act(E["Ea"][:], excl[:], AF.Exp, scale=-ALPHA)
                    P.tt(E["KR"][:, 0, :], E["kap"][:], E["Ea"][:], ALU.mult)
                    P.act(E["Eb"][:], lo[:], AF.Exp, scale=-ALPHA)
                    P.tt(E["kbar"][:], E["kpr"][:], E["Eb"][:], ALU.mult)
                    P.tt(E["bbar"][:], E["b"][:], E["Eb"][:], ALU.mult, eng="pool")
                for c in chunks:
                    cs_ = slice(c * 128, (c + 1) * 128)
                    pY = pb[2]
                    for i in range(2):
                        E = EW[i]
                        pt = pb[4 + i]
                        for qn, nm in enumerate(("vp", "kbar", "bbar")):
                            P.tr(pt[:, qn * 128:(qn + 1) * 128], E[nm][:, cs_], ident)
                        P.cp(TM[i].v(TM[i].t[:].rearrange("p a b -> p (a b)")), pt[:, 0:384], eng="act")
                        pA, pB, pC = pb[0], pb[1], pb[3]
                        for hh in range(2):
                            rs = slice(hh * 64, (hh + 1) * 64)
                            krr = V(E["KR"].t[rs, :, cs_], E["KR"].res)
                            P.mm(V(pA.t[:, hh * 256:(hh + 1) * 256].rearrange("p (a b) -> p a b", a=2), pA.res),
                                 E["khat"][rs, cs_], krr)
                            P.mm(V(pB.t[:, hh * 256:(hh + 1) * 256].rearrange("p (a b) -> p a b", a=2), pB.res),
                                 E["bhat"][rs, cs_], krr)
                            P.mm(pC[:, hh * 128:(hh + 1) * 128], E["KR"][rs, 0, cs_], E["bhat"][rs, cs_])
                        fl = lambda t: t.v(t.t[:].rearrange("p a b c -> p (a b c)"))
                        P.tt(fl(MA[i]), pA[:, :], fl(maskMA), ALU.mult)
                        P.tt(fl(NB[i]), pB[:, :], fl(maskNB), ALU.mult)
                        P.tt(Xa[0][:, 2 * i:2 * i + 2, :], V(pC.t[:, 0:256].rearrange("p (a b) -> p a b", a=2), pC.res),
                             maskX[:], ALU.mult)
                        for hh in range(2):
                            h = 2 * i + hh
                            rs = slice(hh * 64, (hh + 1) * 64)
                            P.cp(XTa[0][:, h, :], NB[i][:, hh, 0, :], eng="pool")
                            P.tr(pY[:, h * 128:h * 128 + 64], E["KR"][rs, 0, cs_], ident[rs, rs] if False else V(cmat.t[rs, CM_ID, hh * 64:(hh + 1) * 64], cmat.res))
                            P.mm(pY[:, h * 128 + 64:h * 128 + 128], MA[i][:, hh, 0, :], TM[i][:, 0, hh * 64:(hh + 1) * 64])
                    Yf = Y.v(Y.t[:].rearrange("p a b -> p (a b)"))
                    P.cp(Yf, pY[:, :])
                    cur = 0
                    for lev in range(7):
                        pAp = pb[0]
                        for h in range(4):
                            P.mm(pAp[:, h * 128:(h + 1) * 128], XTa[cur][:, h, :], Y[:, h, :])
                        if lev < 6:
                            p1, p2 = pb[1], pb[3]
                            for h in range(4):
                                P.mm(p1[:, h * 128:(h + 1) * 128], XTa[cur][:, h, :], Xa[cur][:, h, :])
                                P.mm(p2[:, h * 128:(h + 1) * 128], Xa[cur][:, h, :], XTa[cur][:, h, :])
                            nx = 1 - cur
                            P.cp(Xa[nx].v(Xa[nx].t[:].rearrange("p a b -> p (a b)")), p1[:, :], eng="act")
                            P.cp(XTa[nx].v(XTa[nx].t[:].rearrange("p a b -> p (a b)")), p2[:, :], eng="dve")
                            P.tt(Yf, pAp[:, :], Yf, ALU.add)
                            cur = nx
                        else:
                            for w in range(2):
                                P.stt(NU[:, w, :, :], V(pAp.t[:, :].rearrange("p (h w j) -> p h w j", h=4, w=2)[:, :, w, :], pAp.res),
                                      -1.0, V(Y.t[:, :, w * 64:(w + 1) * 64], Y.res), ALU.mult, ALU.subtract)
                    for i in range(2):
                        cb = 2 * quad + i
                        E = EW[i]
                        pQ, pO, pG, pD = pb[4], pb[5], pb[6], pb[7]
                        nUk2 = V(NU.t[:, 0, 2 * i:2 * i + 2, :].rearrange("p a b -> p (a b)"), NU.res)
                        nU02 = NU[:, 1, 2 * i:2 * i + 2, :]
                        P.mm(V(pQ.t[:, 0:256].rearrange("p (a b) -> p a b", a=2), pQ.res), nUk2, NB[i][:, :, 1, :])
                        for hh in range(2):
                            h = 2 * i + hh
                            rs = slice(hh * 64, (hh + 1) * 64)
                            P.mm(pO[:, i * 128 + hh * 64:i * 128 + (hh + 1) * 64], MA[i][:, hh, 1, :], TM[i][:, 0, rs], start=True, stop=False)
                            P.mm(pO[:, i * 128 + hh * 64:i * 128 + (hh + 1) * 64], NB[i][:, hh, 1, :], NU[:, 1, h, :], start=False, stop=True)
                        P.mm(pG[:, 0:128], nUk2, TM[i][:, 2, :])
                        P.mm(V(pD.t[:, 0:128].rearrange("p (a b) -> p a b", a=2), pD.res), TM[i][:, 1, :], V(TM[i].t[:, 0, :].rearrange("p (a b) -> p a b", a=2), TM[i].res), start=True, stop=False)
                        P.mm(V(pD.t[:, 0:128].rearrange("p (a b) -> p a b", a=2), pD.res), TM[i][:, 2, :], nU02, start=False, stop=True)
                        for hh in range(2):
                            rs = slice(hh * 64, (hh + 1) * 64)
                            fs = slice(hh * 64, (hh + 1) * 64)
                            P.tt(QE[rs, i, :], pQ[rs, hh * 128:(hh + 1) * 128], E["KR"][rs, 1, cs_], ALU.add)
                            pcol_pc = E["PC"][rs, c:c + 1]
                            P.stt(GTbd[rs, i, fs], V(cmat.t[rs, CM_ID, fs], cmat.res), pcol_pc, pG[rs, fs], ALU.mult, ALU.add)
                            P.cp(dHbd[rs, i, fs], pD[rs, fs], eng="act")
                        pS, pH = pb[0], pb[1]
                        P.mm(pS[:, i * 128:(i + 1) * 128], QE[:, i, :], Hbd[:, cb, :])
                        P.mm(pH[:, i * 128:(i + 1) * 128], GTbd[:, i, :], Hbd[:, cb, :])
                        P.cp(o_sb[c][:, cb * 128:(cb + 1) * 128], pO[:, i * 128:(i + 1) * 128], eng="act")
                        P.tt(o_sb[c][:, cb * 128:(cb + 1) * 128], pS[:, i * 128:(i + 1) * 128],
                             o_sb[c][:, cb * 128:(cb + 1) * 128], ALU.add)
                        P.tt(Hbd[:, cb, :], pH[:, i * 128:(i + 1) * 128], dHbd[:, i, :], ALU.add)
            for c in range(2):
                ti = tiles[c]
                rows = slice(ti * 128, (ti + 1) * 128)
                if d == 1:
                    P.dma(o_prev[:], io["orw"].v(ti, io["orw"].ap[rows, :]))
                    P.tt(o_sb[c][:], o_sb[c][:], o_prev[:], ALU.add, eng="pool")
                P.dma(io["orw"].v(ti, io["orw"].ap[rows, :]), o_sb[c][:], q="pool")
                if d == 0:
                    cs_ = slice(c * 128, (c + 1) * 128)
                    for hf in range(2):
                        pg = pb[2 + hf]
                        P.mm(pg[:, :], sgd[:, cs_], g2[:, hf * 512:(hf + 1) * 512])
                        P.cp(o_prev[:, hf * 512:(hf + 1) * 512], pg[:, :], eng="act" if hf == 0 else "dve")
                    P.dma(io["gtok"].v(ti, io["gtok"].ap[rows, :]), o_prev[:], q="pool")
                    for hf in range(2):
                        pg = pb[4 + hf]
                        for q4 in range(4):
                            cb = hf * 4 + q4
                            P.tr(pg[:, q4 * 128:(q4 + 1) * 128], bon[:, cb, cs_], ident)
                        P.cp(o_prev[:, hf * 512:(hf + 1) * 512], pg[:, :], eng="act" if hf == 0 else "dve")
                    P.dma(io["bonus"].v(ti, io["bonus"].ap[rows, :]), o_prev[:], q="pool")
        P.barrier()
    P.stack = None


SCRATCH = {"modrow": lambda c: [2, 6 * D], "orw": lambda c: [c.NT, D], "gtok": lambda c: [c.NT, D],
           "bonus": lambda c: [c.NT, D], "ogl": lambda c: [c.NT, D], "ort": lambda c: [c.NT, D],
           "mrg": lambda c: [c.NT, D], "xmid": lambda c: [c.NT, D]}


def build_layer_program(cfg, dbg=None):
    nc = bass.Bass("TRN2", target_bir_lowering=False)
    io = {}

    def din(name, shape):
        io[name] = DT(nc.dram_tensor(name, list(shape), F32, kind="ExternalInput").ap(), name)

    for k, shp in LAYER_IN_SHAPES.items():
        din(k, shp)
    din("xin", [cfg.NT, D])
    din("vfin", [8, 128, cfg.NT])
    din("cmat", [128, 8 * 128])
    din("cosT", [128, cfg.NT])
    din("sinT", [128, cfg.NT])
    din("cosK", [cfg.NT, 128])
    din("sinK", [cfg.NT, 128])
    for k in ("xout",):
        io[k] = DT(nc.dram_tensor(k, [cfg.NT, D], F32, kind="ExternalOutput").ap(), k)
    io["vout"] = DT(nc.dram_tensor("vout", [8, 128, cfg.NT], F32, kind="ExternalOutput").ap(), "vout")
    for k, f in SCRATCH.items():
        kind = "ExternalOutput" if dbg else "Internal"
        io[k] = DT(nc.dram_tensor(k, f(cfg), F32, kind=kind).ap(), k)
    P = Prog(nc)
    emit_layer(P, cfg, io, dbg=dbg)
    P.finish()
    return nc, P


def tile_order(cfg, d):
    ctx = list(range(cfg.NCT))
    lat = list(range(cfg.NCT, cfg.NTL))
    return (ctx + lat) if d == 0 else (ctx[::-1] + lat[::-1])


def emit_gla(L, d):
    P, cfg, io, pb, pcol, cmat = L.P, L.cfg, L.io, L.pb, L.pcol, L.cmat
    O_GLA = 3392
    with ExitStack() as st:
        P.stack = st
        Wg = P.sb("Wg", [128, 8, 2112], BF16)
        with ExitStack() as st2:
            P.stack = st2
            stg = Rot([P.sb("wstg%d" % i, [128, 1024]) for i in range(2)])
            load_cast(P, Wg, io["w_in"], 0, D, O_GLA, 2048, stg)
            load_cast(P, Wg, io["w_ad"], 0, D, 0, 64, stg, d0=2048)
            P.barrier()
        P.stack = st
        ga2 = P.sb("ga2", [64, 512]); P.memset(ga2[:], 0.0)
        P.dma(ga2[d * 32:d * 32 + 16, :], io["gla_a2"].v(0, io["gla_a2"].ap[d * 32:d * 32 + 16, :]))
        gab = P.sb("gab", [1, 512]); P.dma(gab[:], io["gla_ab"].v(0, io["gla_ab"].ap[0:1, d * 512:(d + 1) * 512]))
        cI = CM_IF if d == 0 else CM_IB
        cK = CM_SB if d == 0 else CM_SF
        mask4 = P.sb("mask4", [128, 4, 128])
        for h in range(4):
            P.cp(mask4[:, h, :], cmat[:, cI, :])
        S = P.sb("Sg", [128, 4, 256]); P.memset(S[:], 0.0)
        Sbf = P.sb("Sgbf", [128, 4, 256], BF16); P.memset(Sbf[:], 0.0)
        hT1 = P.sb("hT1", [128, 8, 128], BF16)
        bufs = {"xt": Rot([P.sb("xt0", [128, D])]), "xn": Rot([P.sb("xn0", [128, D])]),
                "ss": Rot([P.sb("ss%d" % i, [128, 4]) for i in range(2)]), "ps": Rot([pb[6], pb[7]])}
        adT = P.sb("adT", [64, 128])
        v_sb = P.sb("v_sb", [128, D], BF16)
        e1 = P.sb("e1", [128, 512]); sp = P.sb("sp", [128, 512])
        Eq = P.sb("Eq", [128, 4, 128]); Ek = P.sb("Ek", [128, 4, 128])
        qin = P.sb("qin", [128, 4, 128], BF16); kin = P.sb("kin", [128, 4, 128], BF16)
        koe = P.sb("koe", [128, 512]); kout = P.sb("kout", [128, 512], BF16)
        sc_sb = P.sb("sc_sb", [128, 4, 128], BF16)
        o_sb = P.sb("o_sb", [128, D]); o_prev = P.sb("o_prev", [128, D])
        for ti in tile_order(cfg, d):
            rows = slice(ti * 128, (ti + 1) * 128)
            build_hT(L, hT1[:, :, :], ti, L.G1, L.S1, io["xin"], bufs)
            pq_, pk_ = pb[0], pb[1]
            for blk in range(8):
                dst = pq_ if blk < 4 else pk_
                for kb in range(8):
                    P.mm(dst[:, (blk % 4) * 128:(blk % 4 + 1) * 128], Wg[:, kb, blk * 128:(blk + 1) * 128], hT1[:, kb, :],
                         start=(kb == 0), stop=(kb == 7))
            pa = pb[2]
            for kb in range(8):
                P.mm(pa[0:64, 0:128], Wg[:, kb, 2048:2112], hT1[:, kb, :], start=(kb == 0), stop=(kb == 7))
            P.cp(adT[:], pa[0:64, 0:128], eng="act")
            for hf in range(2):
                pv = pb[3 + hf]
                for kb in range(8):
                    P.mm(pv[:, :], hT1[:, kb, :], Wg[:, kb, 1024 + hf * 512:1024 + (hf + 1) * 512], start=(kb == 0), stop=(kb == 7))
                P.cp(v_sb[:, hf * 512:(hf + 1) * 512], pv[:, :], eng="act" if hf == 0 else "dve")
            pg = pb[2]
            P.mm(pg[:, :], adT[:, :], ga2[:, :], start=True, stop=False)
            P.mm(pg[:, :], L.ones[0:1, 0:128], gab[:, :], start=False, stop=True)
            P.act(e1[:], pg[:, :], AF.Exp, scale=-1.0)
            P.act(sp[:], e1[:], AF.Ln, bias=L.ones[:, 0:1])
            pc_ = pb[3]
            for h in range(4):
                P.mm(pc_[:, h * 128:(h + 1) * 128], sp[:, h * 128:(h + 1) * 128], cmat[:, cI, :])
            fl = lambda t: t.v(t.t[:].rearrange("p a b -> p (a b)"))
            P.act(fl(Eq), pc_[:, :], AF.Exp, scale=-1.0 / 16.0)
            P.act(fl(Ek), pc_[:, :], AF.Exp, scale=1.0 / 16.0)
            P.stt(fl(qin), pq_[:, :], float(128 ** -0.5), fl(Eq), ALU.mult, ALU.mult)
            P.tt(fl(kin), pk_[:, :], fl(Ek), ALU.mult)
            pko = pb[4]
            P.mm(pko[:, :], cmat[:, cK, :], sp[:, :])
            P.act(koe[:], pko[:, :], AF.Exp, scale=-1.0 / 16.0)
            pkt = pb[5]
            for kb in range(8):
                P.mm(pkt[:, :], hT1[:, kb, :], Wg[:, kb, 512:1024], start=(kb == 0), stop=(kb == 7))
            P.tt(kout[:], pkt[:, :], koe[:], ALU.mult)
            psc = pb[0]
            for h in range(4):
                P.mm(psc[:, h * 128:(h + 1) * 128], kin[:, h, :], qin[:, h, :])
            P.tt(fl(sc_sb), psc[:, :], fl(mask4), ALU.mult)
            for hp in range(2):
                po = pb[1 + hp]
                for hh in range(2):
                    h = hp * 2 + hh
                    P.mm(po[:, hh * 256:(hh + 1) * 256], sc_sb[:, h, :], v_sb[:, h * 256:(h + 1) * 256], start=True, stop=False)
                    P.mm(po[:, hh * 256:(hh + 1) * 256], qin[:, h, :], Sbf[:, h, :], start=False, stop=True)
                P.cp(o_sb[:, hp * 512:(hp + 1) * 512], po[:, :], eng="act" if hp == 0 else "dve")
            for hp in range(2):
                pkv = pb[3 + hp]
                for hh in range(2):
                    h = hp * 2 + hh
                    P.mm(pkv[:, hh * 256:(hh + 1) * 256], kout[:, h * 128:(h + 1) * 128], v_sb[:, h * 256:(h + 1) * 256])
                for hh in range(2):
                    h = hp * 2 + hh
                    dcol = Eq[:, h, 127:128] if d == 0 else Eq[:, h, 0:1]
                    P.stt(S[:, h, :], S[:, h, :], dcol, pkv[:, hh * 256:(hh + 1) * 256], ALU.mult, ALU.add)
            P.cp(fl(Sbf), fl(S), eng="pool")
            if d == 1:
                P.dma(o_prev[:], io["ogl"].v(ti, io["ogl"].ap[rows, :]))
                P.tt(o_sb[:], o_sb[:], o_prev[:], ALU.add, eng="pool")
            P.dma(io["ogl"].v(ti, io["ogl"].ap[rows, :]), o_sb[:], q="pool")
        P.barrier()
    P.stack = None


def emit_ret(L, d):
    P, cfg, io, pb, pcol, cmat = L.P, L.cfg, L.io, L.pb, L.pcol, L.cmat
    O_RET = 6496
    KS = float(256 ** -0.5)
    with ExitStack() as st:
        P.stack = st
        Wt = P.sb("Wt", [128, 8, 3072], BF16)
        with ExitStack() as st2:
            P.stack = st2
            stg = Rot([P.sb("wstg%d" % i, [128, 1024]) for i in range(2)])
            load_cast(P, Wt, io["w_in"], 0, D, O_RET, 3072, stg)
            P.barrier()
        P.stack = st
        fl = lambda t: t.v(t.t[:].rearrange("p a b -> p (a b)"))
        lgc = P.sb("lgc", [128, 4])
        P.act(lgc[:], pcol[:, PC_RETD + d * 4:PC_RETD + d * 4 + 4], AF.Exp)
        P.ts(lgc[:], lgc[:], -1.0, ALU.mult)
        cD, cI = (CM_DF, CM_IF) if d == 0 else (CM_DB, CM_IB)
        Dm = P.sb("Dm", [128, 4, 128])
        qs = P.sb("qs", [128, 4]); ks = P.sb("ks", [128, 4]); dS = P.sb("dS", [128, 4])
        c128 = P.sb("c128", [128, 1]); P.memset(c128[:], 128.0)
        for h in range(4):
            P.act(Dm[:, h, :], cmat[:, cD, :], AF.Exp, scale=lgc[:, h:h + 1])
            P.tt(Dm[:, h, :], Dm[:, h, :], cmat[:, cI, :], ALU.mult)
            P.act(qs[:, h:h + 1], pcol[:, (PC_IDX1 if d == 0 else PC_RIDX1):(PC_IDX1 if d == 0 else PC_RIDX1) + 1], AF.Exp, scale=lgc[:, h:h + 1])
            P.act(ks[:, h:h + 1], pcol[:, (PC_RIDX if d == 0 else PC_IDX):(PC_RIDX if d == 0 else PC_IDX) + 1], AF.Exp, scale=lgc[:, h:h + 1])
            P.act(dS[:, h:h + 1], c128[:], AF.Exp, scale=lgc[:, h:h + 1])
        S = P.sb("Sr", [128, 8, 256]); P.memset(S[:], 0.0)
        Sbf = P.sb("Srbf", [128, 8, 256], BF16); P.memset(Sbf[:], 0.0)
        hT1 = P.sb("hT1", [128, 8, 128], BF16)
        bufs = {"xt": Rot([P.sb("xt0", [128, D])]), "xn": Rot([P.sb("xn0", [128, D])]),
                "ss": Rot([P.sb("ss%d" % i, [128, 4]) for i in range(2)]), "ps": Rot([pb[6], pb[7]])}
        cosf = P.sb("cosf", [128, 128]); sinf = P.sb("sinf", [128, 128])
        cosk = P.sb("cosk", [128, 128]); sink = P.sb("sink", [128, 128])
        cost = P.sb("cost", [128, 128]); sint = P.sb("sint", [128, 128])
        qr = P.sb("qr", [128, 8, 128], BF16); kr = P.sb("kr", [128, 8, 128], BF16)
        tA = P.sb("tA", [128, 4, 128]); tB = P.sb("tB", [128, 4, 128])
        v_sb = P.sb("v_sb", [128, D], BF16)
        ktk = P.sb("ktk", [128, 4, 2, 128]); kout = P.sb("kout", [128, D], BF16)
        sc_sb = P.sb("sc_sb", [128, 4, 128], BF16)
        o_sb = P.sb("o_sb", [128, D]); o_prev = P.sb("o_prev", [128, D])

        def bc(t, n):
            return V(t.t[:, :].unsqueeze(1).to_broadcast([128, n, 128]), t.res)

        for ti in tile_order(cfg, d):
            rows = slice(ti * 128, (ti + 1) * 128)
            build_hT(L, hT1[:, :, :], ti, L.G1, L.S1, io["xin"], bufs)
            P.dma(cosf[:], io["cosT"].v(0, io["cosT"].ap[:, rows]))
            P.dma(sinf[:], io["sinT"].v(0, io["sinT"].ap[:, rows]), q="pool")
            P.dma(cost[:], io["cosK"].v(0, io["cosK"].ap[rows, :]))
            P.dma(sint[:], io["sinK"].v(0, io["sinK"].ap[rows, :]), q="pool")
            P.ts(cosk[:], cosf[:], KS, ALU.mult, eng="pool")
            P.ts(sink[:], sinf[:], KS, ALU.mult, eng="pool")
            for which in range(2):
                dstT = qr if which == 0 else kr
                cc, sn = (cosf, sinf) if which == 0 else (cosk, sink)
                for bk in range(2):
                    ps = pb[which * 2 + bk]
                    for q4 in range(4):
                        blk = bk * 4 + q4
                        col0 = which * 1024 + blk * 128
                        for kb in range(8):
                            P.mm(ps[:, q4 * 128:(q4 + 1) * 128], Wt[:, kb, col0:col0 + 128], hT1[:, kb, :],
                                 start=(kb == 0), stop=(kb == 7))
                    pv = ps.t[:, :].rearrange("p (h w t) -> p h w t", h=2, w=2)
                    t1 = V(pv[:, :, 0, :], ps.res); t2 = V(pv[:, :, 1, :], ps.res)
                    dv = dstT.t[:, bk * 4:(bk + 1) * 4, :].rearrange("p (h w) t -> p h w t", w=2)
                    a_ = V(tA.t[:, 0:2, :], tA.res); b_ = V(tA.t[:, 2:4, :], tA.res)
                    c_ = V(tB.t[:, 0:2, :], tB.res); d_ = V(tB.t[:, 2:4, :], tB.res)
                    P.tt(a_, t1, bc(cc, 2), ALU.mult)
                    P.tt(b_, t2, bc(sn, 2), ALU.mult)
                    P.tt(V(dv[:, :, 0, :], dstT.res), a_, b_, ALU.subtract, eng="pool")
                    P.tt(c_, t1, bc(sn, 2), ALU.mult)
                    P.tt(d_, t2, bc(cc, 2), ALU.mult)
                    P.tt(V(dv[:, :, 1, :], dstT.res), c_, d_, ALU.add, eng="pool")
            for hf in range(2):
                pv_ = pb[4 + hf]
                for kb in range(8):
                    P.mm(pv_[:, :], hT1[:, kb, :], Wt[:, kb, 2048 + hf * 512:2048 + (hf + 1) * 512], start=(kb == 0), stop=(kb == 7))
                P.cp(v_sb[:, hf * 512:(hf + 1) * 512], pv_[:, :], eng="act" if hf == 0 else "dve")
            for hf in range(2):
                pk_ = pb[4 + hf]
                for kb in range(8):
                    P.mm(pk_[:, :], hT1[:, kb, :], Wt[:, kb, 1024 + hf * 512:1024 + (hf + 1) * 512], start=(kb == 0), stop=(kb == 7))
                pv = pk_.t[:, :].rearrange("p (h w t) -> p h w t", h=2, w=2)
                t1 = V(pv[:, :, 0, :], pk_.res); t2 = V(pv[:, :, 1, :], pk_.res)
                a_ = V(tA.t[:, 0:2, :], tA.res); b_ = V(tA.t[:, 2:4, :], tA.res)
                c_ = V(tB.t[:, 0:2, :], tB.res); d_ = V(tB.t[:, 2:4, :], tB.res)
                P.tt(a_, t1, bc(cost, 2), ALU.mult)
                P.tt(b_, t2, bc(sint, 2), ALU.mult)
                P.tt(ktk[:, hf * 2:hf * 2 + 2, 0, :], a_, b_, ALU.subtract, eng="pool")
                P.tt(c_, t1, bc(sint, 2), ALU.mult)
                P.tt(d_, t2, bc(cost, 2), ALU.mult)
                P.tt(ktk[:, hf * 2:hf * 2 + 2, 1, :], c_, d_, ALU.add, eng="pool")
            for h in range(4):
                P.ts(kout[:, h * 256:(h + 1) * 256], V(ktk.t[:, h, :, :].rearrange("p a b -> p (a b)"), ktk.res),
                     ks[:, h:h + 1], ALU.mult, KS, ALU.mult)
            psc = pb[0]
            for h in range(4):
                for w in range(2):
                    P.mm(psc[:, h * 128:(h + 1) * 128], kr[:, 2 * h + w, :], qr[:, 2 * h + w, :], start=(w == 0), stop=(w == 1))
            P.tt(fl(sc_sb), psc[:, :], fl(Dm), ALU.mult)
            for hp in range(2):
                pin, pit = pb[1], pb[2]
                for hh in range(2):
                    h = hp * 2 + hh
                    P.mm(pin[:, hh * 256:(hh + 1) * 256], sc_sb[:, h, :], v_sb[:, h * 256:(h + 1) * 256])
                    for w in range(2):
                        P.mm(pit[:, hh * 256:(hh + 1) * 256], qr[:, 2 * h + w, :], Sbf[:, 2 * h + w, :], start=(w == 0), stop=(w == 1))
                P.cp(o_sb[:, hp * 512:(hp + 1) * 512], pin[:, :], eng="act")
                for hh in range(2):
                    h = hp * 2 + hh
                    cs_ = slice(h * 256, (h + 1) * 256)
                    P.stt(o_sb[:, cs_], pit[:, hh * 256:(hh + 1) * 256], qs[:, h:h + 1], o_sb[:, cs_], ALU.mult, ALU.add)
            for blk2 in range(4):
                pkv = pb[3 + blk2 % 2]
                for w in range(2):
                    blk = blk2 * 2 + w
                    h = blk // 2
                    P.mm(pkv[:, w * 256:(w + 1) * 256], kout[:, blk * 128:(blk + 1) * 128], v_sb[:, h * 256:(h + 1) * 256])
                for w in range(2):
                    blk = blk2 * 2 + w
                    h = blk // 2
                    P.stt(S[:, blk, :], S[:, blk, :], dS[:, h:h + 1], pkv[:, w * 256:(w + 1) * 256], ALU.mult, ALU.add)
            P.cp(fl(Sbf), fl(S), eng="pool")
            if d == 1:
                P.dma(o_prev[:], io["ort"].v(ti, io["ort"].ap[rows, :]))
                P.tt(o_sb[:], o_sb[:], o_prev[:], ALU.add, eng="pool")
            P.dma(io["ort"].v(ti, io["ort"].ap[rows, :]), o_sb[:], q="pool")
        P.barrier()
    P.stack = None


def head_norm_tok(L, y, o, nh, eps_t, center, tmp, st):
    P = L.P
    hd = D // nh
    for h in range(nh):
        sl = slice(h * hd, (h + 1) * hd)
        if center:
            P.act(tmp[:, sl], o[:, sl], AF.Identity, accum=st[:, h:h + 1])
        P.act(tmp[:, sl], o[:, sl], AF.Square, accum=st[:, 16 + h:17 + h])
    if center:
        P.ts(st[:, 0:nh], st[:, 0:nh], 1.0 / hd, ALU.mult)
        P.tt(st[:, 32:32 + nh], st[:, 0:nh], st[:, 0:nh], ALU.mult)
        P.stt(st[:, 16:16 + nh], st[:, 16:16 + nh], 1.0 / hd, st[:, 32:32 + nh], ALU.mult, ALU.subtract)
        P.ts(st[:, 16:16 + nh], st[:, 16:16 + nh], 0.0, ALU.max)
        P.act(st[:, 16:16 + nh], st[:, 16:16 + nh], AF.Sqrt, bias=eps_t[:])
    else:
        P.act(st[:, 16:16 + nh], st[:, 16:16 + nh], AF.Sqrt, scale=1.0 / hd, bias=eps_t[:])
    P.recip(st[:, 32:32 + nh], st[:, 16:16 + nh])
    for h in range(nh):
        sl = slice(h * hd, (h + 1) * hd)
        if center:
            P.ts(y[:, sl], o[:, sl], st[:, h:h + 1], ALU.subtract, st[:, 32 + h:33 + h], ALU.mult,
                 eng="dve" if h % 2 == 0 else "pool")
        else:
            P.ts(y[:, sl], o[:, sl], st[:, 32 + h:33 + h], ALU.mult, eng="dve" if h % 2 == 0 else "pool")


def emit_post(L):
    P, cfg, io, pb, pcol, cmat = L.P, L.cfg, L.io, L.pb, L.pcol, L.cmat
    ident = L.ident
    O_GLA_R = 3392 + 2048
    O_RET_G = 6496 + 3072
    O_GATE = 10592
    with ExitStack() as st:
        P.stack = st
        Wga = P.sb("Wga", [128, 8, 3072], BF16)
        Wrg = P.sb("Wrg", [128, 8, 2048], BF16)
        Wb = P.sb("Wb", [128, 24, 1024], BF16)
        with ExitStack() as st2:
            P.stack = st2
            stg = Rot([P.sb("wstg%d" % i, [128, 1024]) for i in range(2)])
            load_cast(P, Wga, io["w_in"], 0, D, O_GATE, 3072, stg)
            load_cast(P, Wrg, io["w_in"], 0, D, O_GLA_R, 1024, stg)
            load_cast(P, Wrg, io["w_in"], 0, D, O_RET_G, 1024, stg, d0=1024)
            for i in range(3):
                for kb in range(8):
                    s_ = stg()
                    P.dma(s_[:, :], io["w_branch"].v(0, io["w_branch"].ap[i, kb * 128:(kb + 1) * 128, :]))
                    P.cp(Wb[:, i * 8 + kb, :], s_[:, :], eng="dve" if kb % 2 == 0 else "pool")
            P.barrier()
        P.stack = st
        rowb = {}
        for nm, pr in (("lng", PR_LNG), ("lnb", PR_LNB), ("glag", PR_GLAG), ("retg", PR_RETG)):
            rowb[nm] = P.sb("rb_" + nm, [128, D])
            P.dma(rowb[nm][:], io["prow"].v(0, io["prow"].ap[pr, :].partition_broadcast(128)))
        hT1 = P.sb("hT1", [128, 8, 128], BF16)
        bufs = {"xt": Rot([P.sb("xt0", [128, D])]), "xn": Rot([P.sb("xn0", [128, D])]),
                "ss": Rot([P.sb("ss%d" % i, [128, 4]) for i in range(2)]), "ps": Rot([pb[6], pb[7]])}
        o_t = P.sb("o_t", [128, D]); aux1 = P.sb("aux1", [128, D]); aux2 = P.sb("aux2", [128, D])
        y_t = P.sb("y_t", [128, D]); tmp = P.sb("tmp", [128, D]); sig = P.sb("sig", [128, D])
        mrg = P.sb("mrg", [128, D]); stt_ = P.sb("stats", [128, 64])
        yT = P.sb("yT", [128, 8, 128], BF16)
        for ti in range(cfg.NTL):
            rows = slice(ti * 128, (ti + 1) * 128)
            build_hT(L, hT1[:, :, :], ti, L.G1, L.S1, io["xin"], bufs)
            for br in range(3):
                for hf in range(2):
                    pg = pb[hf]
                    c0 = br * 1024 + hf * 512
                    for kb in range(8):
                        P.mm(pg[:, :], hT1[:, kb, :], Wga[:, kb, c0:c0 + 512], start=(kb == 0), stop=(kb == 7))
                    P.act(sig[:, hf * 512:(hf + 1) * 512], pg[:, :], AF.Sigmoid)
                if br == 0:
                    P.dma(o_t[:], io["orw"].v(ti, io["orw"].ap[rows, :]))
                    P.dma(aux1[:], io["bonus"].v(ti, io["bonus"].ap[rows, :]), q="pool")
                    P.dma(aux2[:], io["gtok"].v(ti, io["gtok"].ap[rows, :]))
                    head_norm_tok(L, y_t, o_t, 16, L.epsg, True, tmp, stt_)
                    P.tt(y_t[:], y_t[:], rowb["lng"][:], ALU.mult)
                    P.tt(y_t[:], y_t[:], rowb["lnb"][:], ALU.add, eng="pool")
                    P.tt(y_t[:], y_t[:], aux1[:], ALU.add)
                    P.tt(y_t[:], y_t[:], aux2[:], ALU.mult, eng="pool")
                else:
                    src = io["ogl"] if br == 1 else io["ort"]
                    P.dma(o_t[:], src.v(ti, src.ap[rows, :]))
                    for hf in range(2):
                        pg = pb[2 + hf]
                        c0 = (br - 1) * 1024 + hf * 512
                        for kb in range(8):
                            P.mm(pg[:, :], hT1[:, kb, :], Wrg[:, kb, c0:c0 + 512], start=(kb == 0), stop=(kb == 7))
                        P.act(aux1[:, hf * 512:(hf + 1) * 512], pg[:, :], AF.Silu)
                    head_norm_tok(L, y_t, o_t, 4, L.eps5, br == 2, tmp, stt_)
                    P.tt(y_t[:], y_t[:], rowb["glag" if br == 1 else "retg"][:], ALU.mult)
                    P.tt(y_t[:], y_t[:], aux1[:], ALU.mult, eng="pool")
                for half in range(2):
                    ps = pb[4 + half]
                    for q in range(4):
                        kb = half * 4 + q
                        P.tr(ps[:, q * 128:(q + 1) * 128], y_t[:, kb * 128:(kb + 1) * 128], ident)
                    P.cp(V(yT.t[:, half * 4:(half + 1) * 4, :].rearrange("p a b -> p (a b)"), yT.res), ps[:, :],
                         eng="act" if half == 0 else "dve")
                for hf in range(2):
                    pu = pb[2 + hf]
                    for kb in range(8):
                        P.mm(pu[:, :], yT[:, kb, :], Wb[:, br * 8 + kb, hf * 512:(hf + 1) * 512], start=(kb == 0), stop=(kb == 7))
                    cs_ = slice(hf * 512, (hf + 1) * 512)
                    if br == 0:
                        P.tt(mrg[:, cs_], pu[:, :], sig[:, cs_], ALU.mult)
                    else:
                        P.tt(tmp[:, cs_], pu[:, :], sig[:, cs_], ALU.mult)
                        P.tt(mrg[:, cs_], mrg[:, cs_], tmp[:, cs_], ALU.add, eng="pool")
            P.dma(io["mrg"].v(ti, io["mrg"].ap[rows, :]), mrg[:], q="pool")
        P.barrier()
    with ExitStack() as st:
        P.stack = st
        Wo = P.sb("Wo", [128, 8, 1024], BF16)
        W1 = P.sb("W1", [128, 8, 4096], BF16)
        W2 = P.sb("W2", [128, 32, 1024], BF16)
        with ExitStack() as st2:
            P.stack = st2
            stg = Rot([P.sb("wstg%d" % i, [128, 1024]) for i in range(2)])
            load_cast(P, Wo, io["w_out"], 0, D, 0, 1024, stg)
            load_cast(P, W1, io["mlp_w1"], 0, D, 0, 4096, stg)
            load_cast(P, W2, io["mlp_w2"], 0, 4 * D, 0, 1024, stg)
            P.barrier()
        P.stack = st
        A2 = [P.sb("A2_%d" % j, [128, D]) for j in range(2)]
        A5 = [P.sb("A5_%d" % j, [128, D]) for j in range(2)]
        gb = P.sb("gpost", [128, D])
        for (A, off, pr) in ((A2, 2 * D, PR_NPOM), (A5, 5 * D, PR_NPOL)):
            P.dma(gb[:], io["prow"].v(0, io["prow"].ap[pr, :].partition_broadcast(128)))
            for j in range(2):
                P.dma(A[j][:], io["modrow"].v(0, io["modrow"].ap[j, off:off + D].partition_broadcast(128)))
                P.tt(A[j][:], A[j][:], gb[:], ALU.mult)
        xt = P.sb("xt", [128, D]); mg = P.sb("mg", [128, D]); xm = P.sb("xm", [128, D]); xn = P.sb("xn", [128, D])
        tmp = P.sb("tmp", [128, D])
        mT = P.sb("mT", [128, 8, 128], BF16); h2T = P.sb("h2T", [128, 8, 128], BF16)
        hid = P.sb("hid", [128, 32, 128], BF16); rl = P.sb("rl", [128, 512])
        ss = P.sb("ssF", [128, 8])

        def rms_resid(dst, base, pz, A):
            for hf in range(2):
                P.act(tmp[:, hf * 512:(hf + 1) * 512], pz[hf][:, :], AF.Square, accum=ss[:, hf:hf + 1])
            P.tt(ss[:, 2:3], ss[:, 0:1], ss[:, 1:2], ALU.add)
            P.act(ss[:, 3:4], ss[:, 2:3], AF.Sqrt, scale=1.0 / D, bias=L.eps6[:])
            P.recip(ss[:, 4:5], ss[:, 3:4])
            for hf in range(2):
                cs_ = slice(hf * 512, (hf + 1) * 512)
                P.stt(tmp[:, cs_], pz[hf][:, :], ss[:, 4:5], A[:, cs_], ALU.mult, ALU.mult)
                P.tt(dst[:, cs_], base[:, cs_], tmp[:, cs_], ALU.add, eng="pool")

        for ti in range(cfg.NTL):
            rows = slice(ti * 128, (ti + 1) * 128)
            j = 0 if ti < cfg.NCT else 1
            P.dma(xt[:], io["xin"].v(ti, io["xin"].ap[rows, :]))
            P.dma(mg[:], io["mrg"].v(ti, io["mrg"].ap[rows, :]), q="pool")
            for half in range(2):
                ps = pb[half]
                for q in range(4):
                    kb = half * 4 + q
                    P.tr(ps[:, q * 128:(q + 1) * 128], mg[:, kb * 128:(kb + 1) * 128], ident)
                P.cp(V(mT.t[:, half * 4:(half + 1) * 4, :].rearrange("p a b -> p (a b)"), mT.res), ps[:, :],
                     eng="act" if half == 0 else "dve")
            pz = [pb[2], pb[3]]
            for hf in range(2):
                for kb in range(8):
                    P.mm(pz[hf][:, :], mT[:, kb, :], Wo[:, kb, hf * 512:(hf + 1) * 512], start=(kb == 0), stop=(kb == 7))
            rms_resid(xm, xt, pz, A2[j])
            P.act(tmp[:], xm[:], AF.Square, accum=ss[:, 5:6])
            P.act(ss[:, 6:7], ss[:, 5:6], AF.Sqrt, scale=1.0 / D, bias=L.eps6[:])
            P.recip(ss[:, 7:8], ss[:, 6:7])
            P.ts(xn[:], xm[:], ss[:, 7:8], ALU.mult, eng="pool")
            for half in range(2):
                ps = pb[4 + half]
                for q in range(4):
                    kb = half * 4 + q
                    P.tr(ps[:, q * 128:(q + 1) * 128], xn[:, kb * 128:(kb + 1) * 128], ident)
                for q in range(4):
                    kb = half * 4 + q
                    if q % 2 == 0:
                        P.act(h2T[:, kb, :], ps[:, q * 128:(q + 1) * 128], AF.Identity, scale=L.G2[:, kb, j:j + 1], bias=L.S2[:, kb, j:j + 1])
                    else:
                        P.ts(h2T[:, kb, :], ps[:, q * 128:(q + 1) * 128], L.G2[:, kb, j:j + 1], ALU.mult, L.S2[:, kb, j:j + 1], ALU.add)
            for fg in range(8):
                ph = pb[6 + fg % 2]
                for q in range(4):
                    fb = fg * 4 + q
                    for kb in range(8):
                        P.mm(ph[:, q * 128:(q + 1) * 128], W1[:, kb, fb * 128:(fb + 1) * 128], h2T[:, kb, :], start=(kb == 0), stop=(kb == 7))
                P.ts(rl[:], ph[:, :], 0.0, ALU.max)
                P.tt(V(hid.t[:, fg * 4:(fg + 1) * 4, :].rearrange("p a b -> p (a b)"), hid.res), rl[:], rl[:], ALU.mult, eng="pool")
            pm = [pb[0], pb[1]]
            for hf in range(2):
                for fb in range(32):
                    P.mm(pm[hf][:, :], hid[:, fb, :], W2[:, fb, hf * 512:(hf + 1) * 512], start=(fb == 0), stop=(fb == 31))
            rms_resid(xt, xm, pm, A5[j])
            P.dma(io["xout"].v(ti, io["xout"].ap[rows, :]), xt[:], q="pool")
        P.barrier()
    P.stack = None


def build_fused_program(cfg, nl):
    nc = bass.Bass("TRN2", target_bir_lowering=False)
    P = Prog(nc)
    shared = {}

    def dram(name, shape, kind):
        return DT(nc.dram_tensor(name, list(shape), F32, kind=kind).ap(), name)

    for k, shp in (("cmat", [128, 8 * 128]), ("cosT", [128, cfg.NT]), ("sinT", [128, cfg.NT]),
                   ("cosK", [cfg.NT, 128]), ("sinK", [cfg.NT, 128]), ("xin0", [cfg.NT, D]), ("vzero", [8, 128, cfg.NT])):
        shared[k] = dram(k, shp, "ExternalInput")
    for k, f in SCRATCH.items():
        shared[k] = dram(k, f(cfg), "Internal")
    xbuf = [dram("xbuf%d" % i, [cfg.NT, D], "Internal") for i in range(2)]
    vfirst = dram("vfirst", [8, 128, cfg.NT], "Internal")
    vdump = dram("vdump", [8, 128, cfg.NT], "Internal")
    xfinal = dram("xout", [cfg.NT, D], "ExternalOutput")
    for l in range(nl):
        io = dict(shared)
        for k, shp in LAYER_IN_SHAPES.items():
            io[k] = dram("l%d_%s" % (l, k), shp, "ExternalInput")
        io["xin"] = shared["xin0"] if l == 0 else xbuf[(l - 1) % 2]
        io["xout"] = xfinal if l == nl - 1 else xbuf[l % 2]
        io["vfin"] = shared["vzero"] if l == 0 else vfirst
        io["vout"] = vfirst if l == 0 else vdump
        emit_layer(P, cfg, io)
        P.barrier()
    P.finish()
    return nc, P


FUSED = True
N_CORES = 4


def kernel(**inputs):
    cfg = Cfg()
    inputs = {k: np.asarray(v) for k, v in inputs.items()}
    B = inputs["x"].shape[0]
    NL = inputs["w_in"].shape[0]
    consts = host_consts(cfg)
    xs = [np.ascontiguousarray(np.concatenate([inputs["ctx"][b], inputs["x"][b]], 0).astype(np.float32)) for b in range(B)]
    if FUSED:
        nc, P = build_fused_program(cfg, NL)
        in_maps = []
        for core in range(N_CORES):
            b = core % B
            m = dict(consts)
            m["xin0"] = xs[b]
            m["vzero"] = np.zeros((8, 128, cfg.NT), np.float32)
            for l in range(NL):
                for k, v in host_layer_inputs(inputs, l, b, cfg).items():
                    m["l%d_%s" % (l, k)] = v
            in_maps.append(m)
        res = run_bass_kernel_spmd(nc, in_maps, core_ids=list(range(N_CORES)))
        xs = [res.results[b]["xout"] for b in range(B)]
    else:
        nc, P = build_layer_program(cfg)
        vfs = [np.zeros((8, 128, cfg.NT), np.float32) for _ in range(B)]
        for l in range(NL):
            in_maps = []
            for core in range(N_CORES):
                b = core % B
                m = host_layer_inputs(inputs, l, b, cfg)
                m.update(consts)
                m["xin"] = xs[b]
                m["vfin"] = vfs[b]
                in_maps.append(m)
            res = run_bass_kernel_spmd(nc, in_maps, core_ids=list(range(N_CORES)))
            xs = [np.asarray(res.results[b]["xout"]) for b in range(B)]
            if l == 0:
                vfs = [np.asarray(res.results[b]["vout"]) for b in range(B)]
    return np.stack([np.asarray(xs[b])[cfg.NC:] for b in range(B)], 0).astype(np.float32)
```
